# Optimizing a Trainium2 kernel written in Bass

```python
import jax, jax.numpy as jnp
from jax import lax
import numpy as np

D_MODEL = 2048
BATCH = 1
SEQ = 16384
DEPTH = 2

SSM_EXPAND = 2
D_INNER = SSM_EXPAND * D_MODEL
SSM_HEAD_DIM = 64
SSM_HEADS = D_INNER // SSM_HEAD_DIM
SSM_GROUPS = 8
SSM_STATE = 128
CONV_WIDTH = 4
CHUNK = 256
GN = SSM_GROUPS * SSM_STATE
CONV_DIM = D_INNER + 2 * GN
IN_PROJ_DIM = D_INNER + CONV_DIM + SSM_HEADS

N_Q_HEADS = 32
N_KV_HEADS = 4
HEAD_DIM = 64
WINDOW = 128
BLOCK = 128
KV_DIM = N_KV_HEADS * HEAD_DIM

D_FF = -(-8 * D_MODEL // (3 * 256)) * 256

N_A_LAYERS = DEPTH // 2
N_B_LAYERS = DEPTH - N_A_LAYERS
EPS = 1e-6

kernel_name = "yoco_ssd_swa_sink_alibi_sandwich"


def rmsnorm(x, w):
    xf = x.astype(jnp.float32)
    xf = xf * lax.rsqrt(jnp.mean(xf * xf, axis=-1, keepdims=True) + EPS)
    return (xf * w.astype(jnp.float32)).astype(x.dtype)


def causal_depthwise_conv(u, w, b):
    out = lax.conv_general_dilated(
        u, w[:, None, :].astype(u.dtype), window_strides=(1,),
        padding=[(CONV_WIDTH - 1, 0)],
        dimension_numbers=("NWC", "WIO", "NWC"),
        feature_group_count=u.shape[-1])
    return out + b.astype(u.dtype)


def ssd_chunked_scan(x, dt, A, Bm, Cm):
    b, L = x.shape[0], x.shape[1]
    R = SSM_HEADS // SSM_GROUPS
    pad = (-L) % CHUNK
    n_chunks = (L + pad) // CHUNK

    def to_chunks(t):
        t = jnp.pad(t.astype(jnp.float32), [(0, 0), (0, pad)] + [(0, 0)] * (t.ndim - 2))
        t = t.reshape((b, n_chunks, CHUNK) + t.shape[2:])
        return jnp.moveaxis(t, 1, 0)

    xc = to_chunks(x.reshape(b, L, SSM_GROUPS, R, SSM_HEAD_DIM))
    dtc = to_chunks(dt.reshape(b, L, SSM_GROUPS, R))
    Bc = to_chunks(Bm)
    Cc = to_chunks(Cm)
    A_gr = A.astype(jnp.float32).reshape(SSM_GROUPS, R)
    causal = jnp.tril(jnp.ones((CHUNK, CHUNK), dtype=bool))

    def step(state, inp):
        xq, dtq, Bq, Cq = inp
        cum = jnp.moveaxis(jnp.cumsum(dtq * A_gr, axis=1), 1, -1)
        seg = cum[..., :, None] - cum[..., None, :]
        decay = jnp.where(causal, jnp.exp(jnp.where(causal, seg, 0.0)), 0.0)
        xdt = xq * dtq[..., None]
        cb = jnp.einsum("bign,bjgn->bgij", Cq, Bq)
        y_diag = jnp.einsum("bgij,bgrij,bjgrp->bigrp", cb, decay, xdt)
        y_off = jnp.einsum("bign,bgrpn,bgri->bigrp", Cq, state, jnp.exp(cum))
        decay_to_end = jnp.exp(cum[..., -1:] - cum)
        new_state = (state * jnp.exp(cum[..., -1])[..., None, None]
                     + jnp.einsum("bjgn,bgrj,bjgrp->bgrpn", Bq, decay_to_end, xdt))
        return new_state, y_diag + y_off

    state0 = jnp.zeros((b, SSM_GROUPS, R, SSM_HEAD_DIM, SSM_STATE), jnp.float32)
    _, y = lax.scan(step, state0, (xc, dtc, Bc, Cc))
    y = jnp.moveaxis(y, 0, 1).reshape(b, n_chunks * CHUNK, SSM_HEADS, SSM_HEAD_DIM)
    return y[:, :L]


def mamba2_mixer(u, w_in, conv_w, conv_b, dt_bias, A_log, D_skip, norm_w, w_out):
    b, L, _ = u.shape
    proj = u @ w_in
    z, xBC, dt_raw = jnp.split(proj, [D_INNER, D_INNER + CONV_DIM], axis=-1)
    xBC = jax.nn.silu(causal_depthwise_conv(xBC, conv_w, conv_b))
    xs, Bm, Cm = jnp.split(xBC, [D_INNER, D_INNER + GN], axis=-1)
    dt = jax.nn.softplus(dt_raw.astype(jnp.float32) + dt_bias.astype(jnp.float32))
    A = -jnp.exp(A_log.astype(jnp.float32))
    xh = xs.reshape(b, L, SSM_HEADS, SSM_HEAD_DIM)
    y = ssd_chunked_scan(xh, dt, A,
                         Bm.reshape(b, L, SSM_GROUPS, SSM_STATE),
                         Cm.reshape(b, L, SSM_GROUPS, SSM_STATE))
    y = y + D_skip.astype(jnp.float32)[:, None] * xh.astype(jnp.float32)
    y = y.reshape(b, L, D_INNER) * jax.nn.silu(z.astype(jnp.float32))
    yg = y.reshape(b, L, SSM_GROUPS, D_INNER // SSM_GROUPS)
    yg = yg * lax.rsqrt(jnp.mean(yg * yg, axis=-1, keepdims=True) + EPS)
    y = (yg.reshape(b, L, D_INNER) * norm_w.astype(jnp.float32)).astype(u.dtype)
    return y @ w_out


def shared_kv(h, kv_norm_w, w_kv):
    b, L, _ = h.shape
    kv = rmsnorm(h, kv_norm_w) @ w_kv
    k, v = jnp.split(kv, 2, axis=-1)
    return (k.reshape(b, L, N_KV_HEADS, HEAD_DIM), v.reshape(b, L, N_KV_HEADS, HEAD_DIM))


def alibi_slopes():
    h = jnp.arange(1, N_Q_HEADS + 1, dtype=jnp.float32)
    return jnp.exp2(-8.0 * h / N_Q_HEADS)


def sliding_window_sink_attention(u, w_q, k, v, sinks, w_o):
    b, L, _ = u.shape
    R = N_Q_HEADS // N_KV_HEADS
    pad = (-L) % BLOCK
    nb = (L + pad) // BLOCK

    def to_blocks(t):
        t = jnp.pad(t, [(0, 0), (0, pad)] + [(0, 0)] * (t.ndim - 2))
        return t.reshape((b, nb, BLOCK) + t.shape[2:])

    def with_prev(t):
        prev = jnp.concatenate([jnp.zeros_like(t[:, :1]), t[:, :-1]], axis=1)
        return jnp.concatenate([prev, t], axis=2)

    q = (u @ w_q).reshape(b, L, N_KV_HEADS, R, HEAD_DIM)
    qb = to_blocks(q).astype(jnp.float32)
    kw = with_prev(to_blocks(k)).astype(jnp.float32)
    vw = with_prev(to_blocks(v)).astype(jnp.float32)
    scores = jnp.einsum("bnqkrd,bnskd->bnkrqs", qb, kw) * (HEAD_DIM ** -0.5)

    qi = jnp.arange(BLOCK)[:, None]
    sj = jnp.arange(2 * BLOCK)[None, :]
    dist = (BLOCK + qi - sj).astype(jnp.float32)
    key_pos = (jnp.arange(nb)[:, None, None] - 1) * BLOCK + sj[None]
    valid = (dist >= 0) & (dist < WINDOW) & (key_pos >= 0)
    slopes = alibi_slopes().reshape(N_KV_HEADS, R, 1, 1)
    logits = scores - slopes * dist
    logits = jnp.where(valid[None, :, None, None], logits, -jnp.inf)

    sink = sinks.astype(jnp.float32).reshape(1, 1, N_KV_HEADS, R, 1, 1)
    m = jnp.maximum(jnp.max(logits, axis=-1, keepdims=True), sink)
    p = jnp.exp(logits - m)
    denom = jnp.sum(p, axis=-1, keepdims=True) + jnp.exp(sink - m)
    probs = p / denom
    out = jnp.einsum("bnkrqs,bnskd->bnqkrd", probs, vw)
    out = out.reshape(b, nb * BLOCK, N_Q_HEADS * HEAD_DIM)[:, :L].astype(u.dtype)
    return out @ w_o


def swiglu(u, w_gate, w_up, w_down):
    return (jax.nn.silu(u @ w_gate) * (u @ w_up)) @ w_down


def setup_inputs(seed: int = 0) -> dict:
    key = jax.random.key(seed)
    ks = jax.random.split(key, 20)
    f32 = jnp.float32
    nA, nB = N_A_LAYERS, N_B_LAYERS

    def normal(k, shape, scale):
        return jax.random.normal(k, shape, f32) * scale

    x = jax.random.normal(ks[0], (BATCH, SEQ, D_MODEL), f32)
    norm_w = 1.0 + normal(ks[1], (DEPTH, 4, D_MODEL), 0.02)
    ssm_w_in = normal(ks[2], (nA, D_MODEL, IN_PROJ_DIM), D_MODEL ** -0.5)
    ssm_conv_w = normal(ks[3], (nA, CONV_WIDTH, CONV_DIM), CONV_WIDTH ** -0.5)
    ssm_conv_b = normal(ks[4], (nA, CONV_DIM), 0.01)
    dt0 = jnp.exp(jax.random.uniform(ks[5], (nA, SSM_HEADS), f32, np.log(1e-3), np.log(1e-1)))
    ssm_dt_bias = dt0 + jnp.log(-jnp.expm1(-dt0))
    ssm_A_log = jnp.log(jax.random.uniform(ks[6], (nA, SSM_HEADS), f32, 1.0, 16.0))
    ssm_D = 1.0 + normal(ks[7], (nA, SSM_HEADS), 0.02)
    ssm_norm_w = 1.0 + normal(ks[8], (nA, D_INNER), 0.02)
    ssm_w_out = normal(ks[9], (nA, D_INNER, D_MODEL), D_INNER ** -0.5)
    kv_norm_w = 1.0 + normal(ks[10], (D_MODEL,), 0.02)
    w_kv = normal(ks[11], (D_MODEL, 2 * KV_DIM), D_MODEL ** -0.5)
    attn_w_q = normal(ks[12], (nB, D_MODEL, N_Q_HEADS * HEAD_DIM), D_MODEL ** -0.5)
    attn_sinks = normal(ks[13], (nB, N_Q_HEADS), 0.5)
    attn_w_o = normal(ks[14], (nB, N_Q_HEADS * HEAD_DIM, D_MODEL), (N_Q_HEADS * HEAD_DIM) ** -0.5)
    ffn_w_gate = normal(ks[15], (DEPTH, D_MODEL, D_FF), D_MODEL ** -0.5)
    ffn_w_up = normal(ks[16], (DEPTH, D_MODEL, D_FF), D_MODEL ** -0.5)
    ffn_w_down = normal(ks[17], (DEPTH, D_FF, D_MODEL), D_FF ** -0.5)
    return {"x": x, "norm_w": norm_w,
            "ssm_w_in": ssm_w_in, "ssm_conv_w": ssm_conv_w, "ssm_conv_b": ssm_conv_b,
            "ssm_dt_bias": ssm_dt_bias, "ssm_A_log": ssm_A_log, "ssm_D": ssm_D,
            "ssm_norm_w": ssm_norm_w, "ssm_w_out": ssm_w_out,
            "kv_norm_w": kv_norm_w, "w_kv": w_kv,
            "attn_w_q": attn_w_q, "attn_sinks": attn_sinks, "attn_w_o": attn_w_o,
            "ffn_w_gate": ffn_w_gate, "ffn_w_up": ffn_w_up, "ffn_w_down": ffn_w_down}


def reference(x, norm_w, ssm_w_in, ssm_conv_w, ssm_conv_b, ssm_dt_bias, ssm_A_log, ssm_D,
              ssm_norm_w, ssm_w_out, kv_norm_w, w_kv, attn_w_q, attn_sinks, attn_w_o,
              ffn_w_gate, ffn_w_up, ffn_w_down):
    h = x
    k_shared, v_shared = None, None
    for layer in range(DEPTH):
        g = norm_w[layer]
        if layer < N_A_LAYERS:
            i = layer
            mix = mamba2_mixer(rmsnorm(h, g[0]), ssm_w_in[i], ssm_conv_w[i], ssm_conv_b[i],
                               ssm_dt_bias[i], ssm_A_log[i], ssm_D[i], ssm_norm_w[i], ssm_w_out[i])
        else:
            if layer == N_A_LAYERS:
                k_shared, v_shared = shared_kv(h, kv_norm_w, w_kv)
            j = layer - N_A_LAYERS
            mix = sliding_window_sink_attention(rmsnorm(h, g[0]), attn_w_q[j], k_shared, v_shared,
                                                attn_sinks[j], attn_w_o[j])
        h = h + rmsnorm(mix, g[1])
        ffn = swiglu(rmsnorm(h, g[2]), ffn_w_gate[layer], ffn_w_up[layer], ffn_w_down[layer])
        h = h + rmsnorm(ffn, g[3])
    return h
```

```python
from contextlib import ExitStack
import os
import numpy as np
import ml_dtypes
import concourse.bass as bass
import concourse.mybir as mybir
from concourse.bass_utils import run_bass_kernel_spmd

F32 = mybir.dt.float32
F32R = mybir.dt.float32r
BF16 = mybir.dt.bfloat16
AF = mybir.ActivationFunctionType
ALU = mybir.AluOpType
AX = mybir.AxisListType

NCORES = 8
D = 2048
KC = 16
T = 2048
NB = T // 128
TT = 512
NTT = T // TT
CH = 4
DI = 4096
NG = 8
NH = 64
DFF = 5632
FU = DFF // 128
EPS = 1e-6
WIN = 128
NQH = 32
NKV = 4
HD = 64
C_Z, C_X, C_B, C_C, C_DT = 0, 4096, 8192, 9216, 10240
NIN = 10304


class Buf:
    __slots__ = ("name", "w", "r", "excl")

    def __init__(self, name="", excl=False):
        self.name = name
        self.w = None
        self.r = []
        self.excl = excl


def PB(name=""):
    return Buf(name, excl=True)


class KB:
    SEM_LIMIT = 30000

    def __init__(self, nc, es):
        self.nc = nc
        self.es = es
        self.engs = {"pe": nc.tensor, "act": nc.scalar, "dve": nc.vector, "pool": nc.gpsimd, "sp": nc.sync}
        self.sem, self.cnt, self.waited = {}, {}, {}
        self.nsem = 0
        self.dsems = []
        for e in self.engs:
            self._new_sem(e)

    def _new_sem(self, e):
        self.nsem += 1
        self.sem[e] = self.es.enter_context(self.nc.semaphore(f"s{self.nsem}_{e}"))
        self.cnt[e] = 0

    def new_dma_sem(self, name="d"):
        self.nsem += 1
        d = [self.es.enter_context(self.nc.semaphore(f"s{self.nsem}_{name}")), 0]
        self.dsems.append(d)
        return d

    def sbuf(self, name, shape, dtype, es=None):
        self.nsem += 1
        return (es or self.es).enter_context(self.nc.sbuf_tensor(f"sb{self.nsem}_{name}", list(shape), dtype))

    def psum(self, name, shape, dtype=F32, es=None):
        self.nsem += 1
        n = 1
        for d_ in shape[1:]:
            n *= d_
        full = 512 if dtype == F32 else 1024
        assert n <= full
        t = (es or self.es).enter_context(self.nc.psum_tensor(f"ps{self.nsem}_{name}", [128, full], dtype))
        v = t[:, 0:n]
        if len(shape) == 3:
            v = v.rearrange("p (a b) -> p a b", a=shape[1])
        return v

    def _wait(self, e, toks):
        best = {}
        for t in toks:
            if t is None:
                continue
            s, v = t
            k = id(s)
            if k not in best or best[k][1] < v:
                best[k] = (s, v)
        for k, (s, v) in best.items():
            if self.waited.get((e, k), 0) >= v:
                continue
            self.engs[e].wait_ge(s, v)
            self.waited[(e, k)] = v

    @staticmethod
    def _deps(reads, writes, extra):
        toks = list(extra)
        for b in reads:
            toks.append(b.w)
            if b.excl:
                toks.extend(b.r)
        for b in writes:
            toks.append(b.w)
            toks.extend(b.r)
        return toks

    @staticmethod
    def _retire(tok, reads, writes):
        for b in reads:
            b.r.append(tok)
            if len(b.r) > 48:
                b.r = b.r[-48:]
        for b in writes:
            b.w = tok
            b.r = []

    def op(self, e, meth, *args, reads=(), writes=(), deps=(), sig=True, **kw):
        self._wait(e, self._deps(reads, writes, deps))
        inst = getattr(self.engs[e], meth)(*args, **kw)
        tok = None
        if sig:
            if self.cnt[e] >= self.SEM_LIMIT:
                self._new_sem(e)
            self.cnt[e] += 1
            inst.then_inc(self.sem[e], 1)
            tok = (self.sem[e], self.cnt[e])
            self._retire(tok, reads, writes)
        return tok

    def mark(self, tok, reads=(), writes=()):
        self._retire(tok, reads, writes)

    def dma(self, e, dsem, out, in_, reads=(), writes=(), deps=(), **kw):
        self._wait(e, self._deps(reads, writes, deps))
        if dsem[1] >= self.SEM_LIMIT:
            self.nsem += 1
            dsem[0] = self.es.enter_context(self.nc.semaphore(f"s{self.nsem}_dx"))
            dsem[1] = 0
        inst = self.engs[e].dma_start(out=out, in_=in_, **kw)
        dsem[1] += 16
        inst.then_inc(dsem[0], 16)
        tok = (dsem[0], dsem[1])
        self._retire(tok, reads, writes)
        return tok

    def cur(self, e):
        return (self.sem[e], self.cnt[e]) if self.cnt[e] > 0 else None

    def barrier(self, extra=()):
        toks = [self.cur(e) for e in self.engs] + [(d[0], d[1]) for d in self.dsems if d[1] > 0] + list(extra)
        for e in self.engs:
            self._wait(e, toks)


class NS:
    pass


def load_consts(kb, dr):
    C = NS()
    d = kb.new_dma_sem("dc")
    C.b = Buf("consts")
    specs = [("ident", [128, 128], BF16), ("ones_bf", [128, 128], BF16), ("maskT", [128, 128], BF16),
             ("ones_f", [128, 128], F32), ("U_f", [128, 128], F32), ("SL_f", [128, 128], F32),
             ("ncols", [128, 9, 16], F32)]
    toks = []
    for name, shape, dt in specs:
        t = kb.sbuf("c_" + name, shape, dt)
        setattr(C, name, t)
        toks.append(kb.dma("sp", d, t[:], dr[name]))
    C.b.w = toks[-1]
    C.b.w = (d[0], d[1])
    return C


def rms_rstd(kb, C, ps, bps, rstd, brstd, N, nfeat):
    kb.op("dve", "tensor_scalar", out=rstd[:, :N], in0=ps[:, :N], scalar1=1.0 / nfeat, scalar2=EPS,
          op0=ALU.mult, op1=ALU.add, reads=[bps], writes=[brstd])
    kb.op("act", "activation", out=rstd[:, :N], in_=rstd[:, :N], func=AF.Sqrt, reads=[brstd], writes=[brstd])
    kb.op("dve", "reciprocal", out=rstd[:, :N], in_=rstd[:, :N], reads=[brstd], writes=[brstd])


def sumsq_fm(kb, C, sq, bsq, ps, bps, N, nkc=KC):
    tok = None
    for kc in range(nkc):
        tok = kb.op("pe", "matmul", ps[:, :N], C.ones_bf[:], sq[:, kc, :N], start=(kc == 0), stop=(kc == nkc - 1),
                    reads=[bsq, C.b], writes=[bps] if kc == 0 else [], sig=(kc == nkc - 1))
    kb.mark(tok, reads=[bsq], writes=[bps])


class WRing:
    def __init__(self, kb, es, name, nbuf, kcin, width):
        self.kb = kb
        self.n = nbuf
        self.t = [kb.sbuf(f"{name}{i}", [128, kcin, width], BF16, es=es) for i in range(nbuf)]
        self.b = [Buf(f"{name}{i}") for i in range(nbuf)]
        self.d = [kb.new_dma_sem(f"{name}{i}") for i in range(nbuf)]
        self.i = 0

    def load(self, w_ap, kcin, width, eng="pool"):
        i = self.i % self.n
        self.i += 1
        self.kb.dma(eng, self.d[i], self.t[i][:, :kcin, :width], w_ap, writes=[self.b[i]])
        return self.t[i], self.b[i]


def norm_tile(kb, C, S, src, bsrc, N, gidx, out_bf, bout, out_off=0, gidx2=None, out_bf2=None, bout2=None):
    kb.op("act", "activation", out=S.sq[:, :, :N], in_=src[:, :, :N], func=AF.Square, reads=[bsrc], writes=[S.bsq])
    sumsq_fm(kb, C, S.sq, S.bsq, S.psn, S.bpsn, N)
    rms_rstd(kb, C, S.psn, S.bpsn, S.rstd, S.brstd, N, D)
    for kc in range(KC):
        kb.op("dve", "scalar_tensor_tensor", out=out_bf[:, kc, out_off:out_off + N], in0=src[:, kc, :N],
              scalar=C.ncols[:, gidx, kc:kc + 1], in1=S.rstd[:, :N], op0=ALU.mult, op1=ALU.mult,
              reads=[bsrc, S.brstd, C.b], writes=[bout])
        if gidx2 is not None:
            kb.op("dve", "scalar_tensor_tensor", out=out_bf2[:, kc, out_off:out_off + N], in0=src[:, kc, :N],
                  scalar=C.ncols[:, gidx2, kc:kc + 1], in1=S.rstd[:, :N], op0=ALU.mult, op1=ALU.mult,
                  reads=[bsrc, S.brstd, C.b], writes=[bout2])


def post_norm_residual(kb, C, S, mix, bmix, res, bres, N, gidx):
    kb.op("act", "activation", out=S.sq[:, :, :N], in_=mix[:, :, :N], func=AF.Square, reads=[bmix], writes=[S.bsq])
    sumsq_fm(kb, C, S.sq, S.bsq, S.psn, S.bpsn, N)
    rms_rstd(kb, C, S.psn, S.bpsn, S.rstd, S.brstd, N, D)
    for kc in range(KC):
        e = "dve"
        kb.op(e, "scalar_tensor_tensor", out=mix[:, kc, :N], in0=mix[:, kc, :N],
              scalar=C.ncols[:, gidx, kc:kc + 1], in1=S.rstd[:, :N], op0=ALU.mult, op1=ALU.mult,
              reads=[S.brstd, C.b], writes=[bmix])
    kb.op("dve", "tensor_tensor", out=res[:, :, :N], in0=res[:, :, :N], in1=mix[:, :, :N], op=ALU.add,
          reads=[bmix], writes=[bres])


def alloc_norm_scratch(kb, es):
    S = NS()
    S.sq = kb.sbuf("n_sq", [128, KC, TT], BF16, es=es)
    S.bsq = Buf("sq")
    S.psn = kb.psum("n_ps", [128, TT], es=es)
    S.bpsn = PB("psn")
    S.rstd = kb.sbuf("n_rstd", [128, TT], F32, es=es)
    S.brstd = Buf("rstd")
    return S


def gemm_fm(kb, ring, psums, bpsums, act, bact, kcin, N, w_dram, col0, ncols, uw, evac, pi0=0):
    nunit = ncols // uw
    pi = pi0
    for u in range(nunit):
        wt, wb = ring.load(w_dram[u], kcin, uw)
        for s in range(uw // 128):
            ps, bps = psums[pi % len(psums)], bpsums[pi % len(psums)]
            pi += 1
            tok = None
            for kc in range(kcin):
                tok = kb.op("pe", "matmul", ps[:, :N], wt[:, kc, s * 128:(s + 1) * 128], act[:, kc, :N],
                            start=(kc == 0), stop=(kc == kcin - 1), reads=[wb, bact],
                            writes=[bps] if kc == 0 else [], sig=(kc == kcin - 1))
            kb.mark(tok, reads=[wb, bact], writes=[bps])
            evac(u * (uw // 128) + s, ps, bps)
    return pi


def ssm_pass(kb, C, dr, full, dbg=None):
    nc = kb.nc
    with ExitStack() as es:
        u0 = kb.sbuf("u0", [128, KC, CH + T], BF16, es=es)
        bu0 = [Buf(f"u0_{i}") for i in range(NTT + 1)]
        hv = kb.sbuf("hv", [128, 5, 64], F32, es=es)
        bhv = Buf("hv")
        dtm = kb.sbuf("dtm", [128, NB, 64], F32, es=es)
        atm = kb.sbuf("atm", [128, NB, 64], F32, es=es)
        einm = kb.sbuf("einm", [128, NB, 64], F32, es=es)
        w2m = kb.sbuf("w2m", [128, NB, 64], F32, es=es)
        dAm = kb.sbuf("dAm", [128, NB, 64], F32, es=es)
        bpre = [Buf(f"pre{b}") for b in range(NB)]
        dsm = kb.new_dma_sem("dsm")
        kb.dma("sp", dsm, hv[:, 0:3, :], dr["hv"], writes=[bhv])
        kb.op("act", "activation", out=hv[:, 3, :], in_=hv[:, 1, :], func=AF.Exp, reads=[bhv], writes=[bhv])
        kb.op("dve", "tensor_scalar", out=hv[:, 1, :], in0=hv[:, 3, :], scalar1=-1.0, scalar2=None, op0=ALU.mult,
              reads=[bhv], writes=[bhv])

        with ExitStack() as es1:
            S = alloc_norm_scratch(kb, es1)
            xs = [kb.sbuf(f"xs{i}", [128, KC, TT], F32, es=es1) for i in range(2)]
            bxs = [Buf("xs0"), Buf("xs1")]
            dxs = [kb.new_dma_sem("dxs0"), kb.new_dma_sem("dxs1")]
            xTv = dr["xT"].rearrange("(kc p) t -> p kc t", p=128)
            order = [(-1, 0, CH)] + [(i, CH + i * TT, TT) for i in range(NTT)]
            for n, (ti, c0, N) in enumerate(order):
                k = n % 2
                kb.dma("sp", dxs[k], xs[k][:, :, :N], xTv[:, :, c0:c0 + N], writes=[bxs[k]])
                norm_tile(kb, C, S, xs[k], bxs[k], N, 0, u0, bu0[ti + 1], out_off=c0)
            kb.barrier()
        if os.environ.get("MK_STOP") == "1":
            return

        with ExitStack() as es2:
            wdt = kb.sbuf("wdt", [128, KC, 64], BF16, es=es2)
            bwdt = Buf("wdt")
            kb.dma("pool", dsm, wdt[:], dr["w_in_dt"], writes=[bwdt])
            psd = [kb.psum(f"psd{i}", [128, 64], es=es2) for i in range(3)]
            bpsd = [PB(f"psd{i}") for i in range(3)]
            t1 = kb.sbuf("t1", [128, 64], F32, es=es2)
            t2 = kb.sbuf("t2", [128, 64], F32, es=es2)
            t3 = kb.sbuf("t3", [128, 64], F32, es=es2)
            bt = [Buf("t1"), Buf("t2"), Buf("t3")]
            for b in range(NB):
                ti = b // 4
                c0 = CH + b * 128
                tok = None
                for kc in range(KC):
                    tok = kb.op("pe", "matmul", psd[0][:, :], u0[:, kc, c0:c0 + 128], wdt[:, kc, :],
                                start=(kc == 0), stop=(kc == KC - 1), reads=[bu0[ti + 1], bwdt],
                                writes=[bpsd[0]] if kc == 0 else [], sig=(kc == KC - 1))
                kb.mark(tok, reads=[bwdt], writes=[bpsd[0]])
                kb.op("dve", "tensor_tensor", out=t1[:], in0=psd[0][:, :], in1=hv[:, 0, :], op=ALU.add,
                      reads=[bpsd[0], bhv], writes=[bt[0]])
                kb.op("dve", "scalar_tensor_tensor", out=t2[:], in0=t1[:], scalar=-1.0, in1=t1[:], op0=ALU.mult,
                      op1=ALU.max, reads=[bt[0]], writes=[bt[1]])
                kb.op("act", "activation", out=t2[:], in_=t2[:], func=AF.Exp, scale=-1.0, reads=[bt[1]], writes=[bt[1]])
                kb.op("act", "activation", out=t2[:], in_=t2[:], func=AF.Ln, bias=1.0, reads=[bt[1]], writes=[bt[1]])
                kb.op("dve", "scalar_tensor_tensor", out=dtm[:, b, :], in0=t1[:], scalar=0.0, in1=t2[:], op0=ALU.max,
                      op1=ALU.add, reads=[bt[0], bt[1]], writes=[bpre[b]])
                kb.op("dve", "tensor_tensor", out=atm[:, b, :], in0=dtm[:, b, :], in1=hv[:, 1, :], op=ALU.mult,
                      reads=[bpre[b], bhv], writes=[bpre[b]])
                kb.op("pe", "matmul", psd[1][:, :], C.U_f[:], atm[:, b, :], start=True, stop=True,
                      reads=[bpre[b], C.b], writes=[bpsd[1]])
                kb.op("pe", "matmul", psd[2][:, :], C.ones_f[:], atm[:, b, :], start=True, stop=True,
                      reads=[bpre[b], C.b], writes=[bpsd[2]])
                kb.op("act", "activation", out=einm[:, b, :], in_=psd[1][:, :], func=AF.Exp, reads=[bpsd[1]],
                      writes=[bpre[b]])
                kb.op("act", "activation", out=dAm[:, b, :], in_=psd[2][:, :], func=AF.Exp, reads=[bpsd[2]],
                      writes=[bpre[b]])
                kb.op("act", "activation", out=t3[:], in_=psd[1][:, :], func=AF.Identity, reads=[bpsd[1]], writes=[bt[2]])
                kb.op("dve", "tensor_tensor", out=t3[:], in0=psd[2][:, :], in1=t3[:], op=ALU.subtract,
                      reads=[bpsd[2], bt[2]], writes=[bt[2]])
                kb.op("act", "activation", out=t3[:], in_=t3[:], func=AF.Exp, reads=[bt[2]], writes=[bt[2]])
                kb.op("dve", "tensor_tensor", out=w2m[:, b, :], in0=t3[:], in1=dtm[:, b, :], op=ALU.mult,
                      reads=[bt[2], bpre[b]], writes=[bpre[b]])
            if not full:
                kb.op("dve", "tensor_copy", out=t1[:], in_=dAm[:, NB - 1, :], reads=[bpre[NB - 1]], writes=[bt[0]])
                for b in range(NB - 2, -1, -1):
                    kb.op("dve", "tensor_tensor", out=w2m[:, b, :], in0=w2m[:, b, :], in1=t1[:], op=ALU.mult,
                          reads=[bt[0]], writes=[bpre[b]])
                    if b > 0:
                        kb.op("dve", "tensor_tensor", out=t1[:], in0=t1[:], in1=dAm[:, b, :], op=ALU.mult,
                              reads=[bpre[b]], writes=[bt[0]])
            kb.barrier()
        if os.environ.get("MK_STOP") == "2":
            return
        if dbg is not None and "dt" in dbg:
            kb.dma("sp", dsm, dbg["dt"], dtm[:], reads=bpre)
            kb.dma("sp", dsm, dbg["ein"], einm[:], reads=bpre)
            kb.dma("sp", dsm, dbg["w2"], w2m[:], reads=bpre)
            kb.dma("sp", dsm, dbg["dA"], dAm[:], reads=bpre)

        with ExitStack() as es3:
            units = ["x0", "x1", "x2", "x3", "B"] + (["C"] if full else [])
            nun = len(units)
            wu = {n: kb.sbuf(f"w_{n}", [128, KC, 128], BF16, es=es3) for n in units}
            bwu = {n: Buf(f"w_{n}") for n in units}
            dwu = {n: kb.new_dma_sem(f"dw_{n}") for n in units}
            if full:
                wz = kb.sbuf("w_z", [128, KC, 512], BF16, es=es3)
                bwz = Buf("w_z")
                dwz = kb.new_dma_sem("dw_z")
            cw = kb.sbuf("cw", [128, 48, 4], F32, es=es3)
            cbv = kb.sbuf("cbv", [128, 48], F32, es=es3)
            nwc = kb.sbuf("nwc", [128, 32], F32, es=es3)
            bcv = Buf("convw")
            kb.dma("sp", dsm, cw[:], dr["conv_w"], writes=[bcv])
            kb.dma("sp", dsm, cbv[:], dr["conv_b"], writes=[bcv])
            kb.dma("sp", dsm, nwc[:], dr["ssm_nw"], writes=[bcv])
            pre = kb.sbuf("pre", [128, nun, 3 + TT], F32, es=es3)
            bpreu = [Buf(f"preu{i}") for i in range(nun)]
            carry = kb.sbuf("carry", [128, nun, 3], F32, es=es3)
            bcarry = [Buf(f"carry{i}") for i in range(nun)]
            acc = [kb.sbuf(f"acc{i}", [128, TT], F32, es=es3) for i in range(2)]
            bacc = [Buf("acc0"), Buf("acc1")]
            xc = kb.sbuf("xc", [128, nun, TT], BF16, es=es3)
            bxc = [Buf(f"xc{i}") for i in range(nun)]
            St = kb.sbuf("St", [128, 512], F32, es=es3)
            Stb = kb.sbuf("Stb", [128, 512], BF16, es=es3)
            bSt, bStb = Buf("St"), Buf("Stb")
            Stmp = kb.sbuf("Stmp", [128, 512], F32, es=es3)
            bStmp = Buf("Stmp")
            xtm = [kb.sbuf(f"xtm{i}", [128, 512], BF16, es=es3) for i in range(2)]
            xw = [kb.sbuf(f"xw{i}", [128, 512], BF16, es=es3) for i in range(2)]
            btm = [kb.sbuf(f"btm{i}", [128, 128], BF16, es=es3) for i in range(2)]
            bxtm, bxw, bbtm = [Buf(), Buf()], [Buf(), Buf()], [Buf(), Buf()]
            psA = [kb.psum(f"psA{i}", [128, 512], es=es3) for i in range(2)]
            bpsA = [PB("psA0"), PB("psA1")]
            psT = kb.psum("psT", [128, 5, 128], BF16, es=es3)
            bpsT = PB("psT")
            psS = kb.psum("psS", [128, 512], es=es3)
            bpsS = PB("psS")
            psh, bpsh = psS, bpsS
            dout = kb.new_dma_sem("dout")
            dxcc = kb.new_dma_sem("dxcc")
            if full:
                xdt = [kb.sbuf(f"xdt{i}", [128, 512], BF16, es=es3) for i in range(2)]
                bxdt = [Buf(), Buf()]
                zs = kb.sbuf("zs", [128, 4, 512], BF16, es=es3)
                bzs = [Buf(f"zs{i}") for i in range(4)]
                cbm = kb.sbuf("cbm", [128, 128], BF16, es=es3)
                bcbm = Buf("cbm")
                Ua = kb.sbuf("Ua", [128, 8, 128], F32, es=es3)
                bUa = Buf("Ua")
                Ee = kb.sbuf("Ee", [128, 8, 128], BF16, es=es3)
                bEe = Buf("Ee")
                Mt = [kb.sbuf(f"Mt{i}", [128, 8, 128], BF16, es=es3) for i in range(2)]
                bMt = [Buf(), Buf()]
                yd = kb.sbuf("yd", [128, 512], F32, es=es3)
                byd = Buf("yd")
                yt = kb.sbuf("yt", [128, 512], F32, es=es3)
                byt = Buf("yt")
                ynb = kb.sbuf("ynb", [128, 512], BF16, es=es3)
                bynb = Buf("ynb")
                idD = kb.sbuf("idD", [128, 16, 128], BF16, es=es3)
                bidD = Buf("idD")
                dtmp = kb.sbuf("dtmp", [128, 16], F32, es=es3)
                bdtmp = Buf("dtmp")
                dhib = kb.sbuf("dhib", [128, 8], BF16, es=es3)
                psT2 = kb.psum("psT2", [128, 4, 128], BF16, es=es3)
                bpsT2 = PB("psT2")
                junk = kb.sbuf("junk", [128, 512], BF16, es=es3)
                bjunk = Buf("junk")
                st8 = kb.sbuf("st8", [128, 8], F32, es=es3)
                bst8 = Buf("st8")
                ynT = [kb.sbuf(f"ynT{i}", [128, 4, TT], BF16, es=es3) for i in range(2)]
                bynT = [Buf("ynT0"), Buf("ynT1")]
                psE = [kb.psum(f"psE{i}", [128, 512], es=es3) for i in range(2)]
                bpsE = [PB("psE0"), PB("psE1")]
                psY = kb.psum("psY", [128, 512], es=es3)
                bpsY = PB("psY")
            else:
                psE = None
                psAcc = kb.psum("psAcc", [128, 512], es=es3)
                bpsAcc = PB("psAcc")

            for g in range(NG):
                cols = {"x0": C_X + g * 512, "x1": C_X + g * 512 + 128, "x2": C_X + g * 512 + 256,
                        "x3": C_X + g * 512 + 384, "B": C_B + g * 128, "C": C_C + g * 128}
                for ui_, n in enumerate(units):
                    if full and ui_ < 5:
                        continue
                    kb.dma("pool", dwu[n], wu[n][:], dr["w_in_u"][g, ui_], writes=[bwu[n]])
                if full:
                    kb.dma("pool", dwz, wz[:], dr["w_in_z"][g], writes=[bwz])
                    kb.dma("sp", dsm, St[:], dr["Sinit"][g], writes=[bSt])
                    kb.op("act", "copy", out=Stb[:], in_=St[:], reads=[bSt], writes=[bStb])
                    kb.op("dve", "tensor_copy", out=dhib[:], in_=hv[:, 2, g * 8:g * 8 + 8], reads=[bhv], writes=[bdtmp])
                    kb.op("dve", "tensor_copy", out=dtmp[:, 0:8], in_=dhib[:], writes=[bdtmp])
                    kb.op("dve", "tensor_tensor", out=dtmp[:, 8:16], in0=hv[:, 2, g * 8:g * 8 + 8], in1=dtmp[:, 0:8],
                          op=ALU.subtract, reads=[bhv], writes=[bdtmp])
                    for r in range(16):
                        kb.op("act", "activation", out=idD[:, r, :], in_=C.ident[:], func=AF.Identity,
                              scale=dtmp[:, r:r + 1], reads=[C.b, bdtmp], writes=[bidD])
                cunit = [g * 4 + 0, g * 4 + 1, g * 4 + 2, g * 4 + 3, 32 + g, 40 + g]
                for tt in range(NTT):
                    c0 = CH + tt * TT
                    if full:
                        kb.dma("sp", dxcc, xc[:, 0:5, :],
                               dr["xcc"][g, :, :, tt * TT:(tt + 1) * TT].rearrange("u p t -> p u t"),
                               writes=[bxc[0], bxc[1], bxc[2], bxc[3], bxc[4]])
                    for ui, n in enumerate(units):
                        if full and ui < 5:
                            continue
                        if tt == 0:
                            tok = None
                            for kc in range(KC):
                                tok = kb.op("pe", "matmul", psh[:, 0:CH], wu[n][:, kc, :], u0[:, kc, 0:CH],
                                            start=(kc == 0), stop=(kc == KC - 1), reads=[bwu[n], bu0[0]],
                                            writes=[bpsh] if kc == 0 else [], sig=(kc == KC - 1))
                            kb.mark(tok, reads=[bwu[n]], writes=[bpsh])
                            kb.op("act", "copy", out=pre[:, ui, 0:3], in_=psh[:, 1:CH], reads=[bpsh],
                                  writes=[bpreu[ui]])
                        else:
                            kb.op("act", "copy", out=pre[:, ui, 0:3], in_=carry[:, ui, :], reads=[bcarry[ui]],
                                  writes=[bpreu[ui]])
                        ps, bps = psA[ui % 2], bpsA[ui % 2]
                        tok = None
                        for kc in range(KC):
                            tok = kb.op("pe", "matmul", ps[:, :], wu[n][:, kc, :], u0[:, kc, c0:c0 + TT],
                                        start=(kc == 0), stop=(kc == KC - 1), reads=[bwu[n], bu0[tt + 1]],
                                        writes=[bps] if kc == 0 else [], sig=(kc == KC - 1))
                        kb.mark(tok, reads=[bwu[n]], writes=[bps])
                        kb.op("act", "copy", out=pre[:, ui, 3:3 + TT], in_=ps[:, :], reads=[bps], writes=[bpreu[ui]])
                        kb.op("act", "copy", out=carry[:, ui, :], in_=pre[:, ui, TT:TT + 3], reads=[bpreu[ui]],
                              writes=[bcarry[ui]])
                        a_, ba_ = acc[ui % 2], bacc[ui % 2]
                        cu = cunit[ui]
                        eng = "dve"
                        kb.op(eng, "tensor_scalar", out=a_[:], in0=pre[:, ui, 0:TT], scalar1=cw[:, cu, 0:1], scalar2=None,
                              op0=ALU.mult, reads=[bpreu[ui], bcv], writes=[ba_])
                        for k in range(1, 4):
                            kb.op(eng, "scalar_tensor_tensor", out=a_[:], in0=pre[:, ui, k:k + TT],
                                  scalar=cw[:, cu, k:k + 1], in1=a_[:], op0=ALU.mult, op1=ALU.add,
                                  reads=[bpreu[ui], bcv], writes=[ba_])
                        kb.op("act", "activation", out=xc[:, ui, :], in_=a_[:], func=AF.Silu, bias=cbv[:, cu:cu + 1],
                              reads=[ba_, bcv], writes=[bxc[ui]])
                    if not full:
                        kb.dma("sp", dxcc, dr["xcc"][g, :, :, tt * TT:(tt + 1) * TT].rearrange("u p t -> p u t"),
                               xc[:, 0:5, :], reads=[bxc[0], bxc[1], bxc[2], bxc[3], bxc[4]])
                    if full:
                        for q in range(4):
                            ps, bps = psA[q % 2], bpsA[q % 2]
                            tok = None
                            for kc in range(KC):
                                tok = kb.op("pe", "matmul", ps[:, :], u0[:, kc, c0 + q * 128:c0 + (q + 1) * 128],
                                            wz[:, kc, :], start=(kc == 0), stop=(kc == KC - 1),
                                            reads=[bwz, bu0[tt + 1]], writes=[bps] if kc == 0 else [],
                                            sig=(kc == KC - 1))
                            kb.mark(tok, reads=[bwz], writes=[bps])
                            kb.op("act", "activation", out=zs[:, q, :], in_=ps[:, :], func=AF.Silu, reads=[bps],
                                  writes=[bzs[q]])
                    hs = slice(g * 8, g * 8 + 8)

                    def A_T(q):
                        sl = slice(q * 128, (q + 1) * 128)
                        for u4 in range(4):
                            kb.op("pe", "transpose", out=psT[:, u4, :], in_=xc[:, u4, sl], identity=C.ident[:],
                                  reads=[bxc[u4], C.b], writes=[bpsT] if u4 == 0 else [], sig=False)
                        tok = kb.op("pe", "transpose", out=psT[:, 4, :], in_=xc[:, 4, sl], identity=C.ident[:],
                                    reads=[bxc[4], C.b], writes=[])
                        kb.mark(tok, reads=[bxc[0], bxc[1], bxc[2], bxc[3], bxc[4]], writes=[bpsT])

                    def A_ev(q):
                        b = tt * 4 + q
                        k = q % 2
                        psTx = psT[:, 0:4, :].rearrange("p u c -> p (u c)").rearrange("p (r d) -> p r d", r=8)
                        kb.op("act", "copy", out=btm[k][:], in_=psT[:, 4, :], reads=[bpsT], writes=[bbtm[k]])
                        kb.op("dve", "tensor_tensor", out=xw[k][:].rearrange("p (r c) -> p r c", r=8), in0=psTx,
                              in1=w2m[:, b, hs].unsqueeze(2).to_broadcast([128, 8, 64]), op=ALU.mult,
                              reads=[bpsT, bpre[b]], writes=[bxw[k]])
                        if not full:
                            return
                        kb.op("act", "copy", out=xtm[k][:].rearrange("p (u c) -> p u c", u=4), in_=psT[:, 0:4, :],
                              reads=[bpsT], writes=[bxtm[k]])
                        kb.op("dve", "tensor_tensor", out=xdt[k][:].rearrange("p (r c) -> p r c", r=8), in0=psTx,
                              in1=dtm[:, b, hs].unsqueeze(2).to_broadcast([128, 8, 64]), op=ALU.mult,
                              reads=[bpsT, bpre[b]], writes=[bxdt[k]])

                    def A_cb(q):
                        b = tt * 4 + q
                        sl = slice(q * 128, (q + 1) * 128)
                        kb.op("pe", "matmul", psE[0][:, 0:128], xc[:, 4, sl], xc[:, 5, sl], start=True, stop=True,
                              reads=[bxc[4], bxc[5]], writes=[bpsE[0]])
                        kb.op("dve", "tensor_tensor", out=cbm[:], in0=psE[0][:, 0:128], in1=C.maskT[:], op=ALU.mult,
                              reads=[bpsE[0], C.b], writes=[bcbm])
                        kb.op("dve", "tensor_tensor", out=Ua[:], in0=C.U_f[:].unsqueeze(1).to_broadcast([128, 8, 128]),
                              in1=atm[:, b, hs].unsqueeze(2).to_broadcast([128, 8, 128]), op=ALU.mult,
                              reads=[C.b, bpre[b]], writes=[bUa])
                        for h2 in range(2):
                            kb.op("pe", "matmul", psE[h2][:, :], C.SL_f[:], Ua[:, 4 * h2:4 * h2 + 4, :],
                                  start=True, stop=True, reads=[C.b, bUa], writes=[bpsE[h2]])

                    def A_exp(q):
                        k = q % 2
                        for h2 in range(2):
                            kb.op("act", "activation", out=Ee[:, 4 * h2:4 * h2 + 4, :], in_=psE[h2][:, :],
                                  func=AF.Exp, reads=[bpsE[h2]], writes=[bEe])
                        kb.op("dve", "tensor_tensor", out=Mt[k][:], in0=Ee[:],
                              in1=cbm[:].unsqueeze(1).to_broadcast([128, 8, 128]), op=ALU.mult,
                              reads=[bEe, bcbm], writes=[bMt[k]])

                    def B_mm(q):
                        b = tt * 4 + q
                        k = q % 2
                        sl = slice(q * 128, (q + 1) * 128)
                        if not full:
                            kb.op("pe", "matmul", psAcc[:, :], btm[k][:], xw[k][:], start=(b == 0), stop=(b == NB - 1),
                                  reads=[bbtm[k], bxw[k]], writes=[bpsAcc] if b == 0 else [])
                            if b == NB - 1:
                                kb.mark(kb.cur("pe"), writes=[bpsAcc])
                                kb.op("act", "copy", out=St[:], in_=psAcc[:, :], reads=[bpsAcc], writes=[bSt])
                            return
                        for r in range(8):
                            cs_ = slice(r * 64, (r + 1) * 64)
                            kb.op("pe", "matmul", psY[:, cs_], Mt[k][:, r, :], xdt[k][:, cs_], start=True, stop=False,
                                  reads=[bMt[k], bxdt[k]], writes=[bpsY] if r == 0 else [], sig=False)
                            kb.op("pe", "matmul", psY[:, cs_], idD[:, r, :], xtm[k][:, cs_], start=False, stop=False,
                                  reads=[bidD, bxtm[k]], sig=False)
                            tok = kb.op("pe", "matmul", psY[:, cs_], idD[:, 8 + r, :], xtm[k][:, cs_], start=False,
                                        stop=True, reads=[bidD, bxtm[k]], sig=(r == 7))
                        kb.mark(tok, reads=[bMt[k], bxdt[k], bxtm[k], bidD], writes=[bpsY])
                        kb.op("pe", "matmul", psS[:, :], xc[:, 5, sl], Stb[:], start=True, stop=True,
                              reads=[bxc[5], bStb], writes=[bpsS])
                        kb.op("pe", "matmul", psA[0][:, :], btm[k][:], xw[k][:], start=True, stop=True,
                              reads=[bbtm[k], bxw[k]], writes=[bpsA[0]])

                    def B_ew(q):
                        b = tt * 4 + q
                        if not full:
                            return
                        kb.op("dve", "tensor_tensor", out=Stmp[:].rearrange("p (r c) -> p r c", r=8),
                              in0=St[:].rearrange("p (r c) -> p r c", r=8),
                              in1=dAm[:, b, hs].unsqueeze(2).to_broadcast([128, 8, 64]), op=ALU.mult,
                              reads=[bSt, bpre[b]], writes=[bStmp])
                        kb.op("dve", "tensor_tensor", out=St[:], in0=Stmp[:], in1=psA[0][:, :], op=ALU.add,
                              reads=[bStmp, bpsA[0]], writes=[bSt])
                        kb.op("act", "copy", out=yd[:], in_=psY[:, :], reads=[bpsY], writes=[byd])
                        kb.op("act", "copy", out=Stb[:], in_=St[:], reads=[bSt], writes=[bStb])
                        kb.op("dve", "tensor_tensor", out=yt[:].rearrange("p (r c) -> p r c", r=8),
                              in0=psS[:, :].rearrange("p (r c) -> p r c", r=8),
                              in1=einm[:, b, hs].unsqueeze(2).to_broadcast([128, 8, 64]), op=ALU.mult,
                              reads=[bpsS, bpre[b]], writes=[byt])
                        kb.op("dve", "tensor_tensor", out=yt[:], in0=yt[:], in1=yd[:], op=ALU.add,
                              reads=[byd], writes=[byt])
                        kb.op("dve", "tensor_tensor", out=yt[:], in0=yt[:], in1=zs[:, q, :], op=ALU.mult,
                              reads=[bzs[q]], writes=[byt])
                        kb.op("act", "activation", out=junk[:], in_=yt[:], func=AF.Square, accum_out=st8[:, 0:1],
                              reads=[byt], writes=[bjunk, bst8])
                        kb.op("dve", "tensor_scalar", out=st8[:, 1:2], in0=st8[:, 0:1], scalar1=1.0 / 512,
                              scalar2=EPS, op0=ALU.mult, op1=ALU.add, reads=[bst8], writes=[bst8])
                        kb.op("act", "activation", out=st8[:, 1:2], in_=st8[:, 1:2], func=AF.Ln, reads=[bst8],
                              writes=[bst8])
                        kb.op("act", "activation", out=st8[:, 2:3], in_=st8[:, 1:2], func=AF.Exp, scale=-0.5,
                              reads=[bst8], writes=[bst8])
                        kb.op("act", "activation", out=ynb[:], in_=yt[:], func=AF.Identity, scale=st8[:, 2:3],
                              reads=[byt, bst8], writes=[bynb])

                    def B_T2(q):
                        if not full:
                            return
                        sl = slice(q * 128, (q + 1) * 128)
                        k2 = tt % 2
                        for u4 in range(4):
                            tok = kb.op("pe", "transpose", out=psT2[:, u4, :], in_=ynb[:, u4 * 128:(u4 + 1) * 128],
                                        identity=C.ident[:], reads=[bynb, C.b],
                                        writes=[bpsT2] if u4 == 0 else [], sig=(u4 == 3))
                        kb.mark(tok, reads=[bynb], writes=[bpsT2])
                        for u4 in range(4):
                            if u4 % 2 == 0:
                                kb.op("act", "activation", out=ynT[k2][:, u4, sl], in_=psT2[:, u4, :],
                                      func=AF.Identity, scale=nwc[:, g * 4 + u4:g * 4 + u4 + 1],
                                      reads=[bpsT2, bcv], writes=[bynT[k2]])
                            else:
                                kb.op("dve", "tensor_scalar", out=ynT[k2][:, u4, sl], in0=psT2[:, u4, :],
                                      scalar1=nwc[:, g * 4 + u4:g * 4 + u4 + 1], scalar2=None, op0=ALU.mult,
                                      reads=[bpsT2, bcv], writes=[bynT[k2]])

                    A_T(0)
                    A_ev(0)
                    if full:
                        A_cb(0)
                        A_exp(0)
                    for q in range(4):
                        nq = q + 1 < 4
                        if nq:
                            A_T(q + 1)
                            A_ev(q + 1)
                        B_mm(q)
                        if nq and full:
                            A_cb(q + 1)
                        B_ew(q)
                        if nq and full:
                            A_exp(q + 1)
                        B_T2(q)
                    if full:
                        k2 = tt % 2
                        kb.dma("sp", dout, dr["ynT"][g * 512:(g + 1) * 512, tt * TT:(tt + 1) * TT].rearrange(
                            "(u p) t -> p u t", p=128), ynT[k2][:], reads=[bynT[k2]])
                if not full:
                    kb.dma("sp", dout, dr["Sloc"][g], St[:], reads=[bSt])
            if not full:
                kb.dma("sp", dout, dr["dAblk"], dAm[:], reads=bpre)
            kb.barrier()


def prefix_state(kb, C, dr):
    with ExitStack() as es:
        dA = kb.sbuf("pf_dA", [128, NCORES, NB, 64], F32, es=es)
        ws = kb.sbuf("pf_w", [128, 8], F32, es=es)
        dd = kb.sbuf("pf_dd", [128, NCORES, 64], F32, es=es)
        bdA, bws, bdd = Buf("dA"), Buf("ws"), Buf("dd")
        L = [kb.sbuf(f"pf_L{i}", [128, 512], F32, es=es) for i in range(2)]
        bL = [Buf("L0"), Buf("L1")]
        dL = [kb.new_dma_sem("dL0"), kb.new_dma_sem("dL1")]
        Sp = kb.sbuf("pf_S", [128, 512], F32, es=es)
        tmp = kb.sbuf("pf_tmp", [128, 512], F32, es=es)
        bSp, btmp = Buf("Sp"), Buf("tmp")
        d = kb.new_dma_sem("dpf")
        for c in range(NCORES):
            kb.dma("sp", d, dA[:, c], dr["dAall"][c], writes=[bdA])
        kb.dma("sp", d, ws[:], dr["wsel"], writes=[bws])
        kb.op("dve", "tensor_copy", out=dd[:], in_=dA[:, :, 0, :], reads=[bdA], writes=[bdd])
        for b in range(1, NB):
            kb.op("dve", "tensor_tensor", out=dd[:], in0=dd[:], in1=dA[:, :, b, :], op=ALU.mult, reads=[bdA],
                  writes=[bdd])
        for c in range(NCORES):
            kb.op("dve", "tensor_scalar", out=dd[:, c, :], in0=dd[:, c, :], scalar1=-1.0, scalar2=ws[:, c:c + 1],
                  op0=ALU.add, op1=ALU.mult, reads=[bws], writes=[bdd])
        kb.op("dve", "tensor_scalar", out=dd[:], in0=dd[:], scalar1=1.0, scalar2=None, op0=ALU.add, writes=[bdd])
        n = 0
        for g in range(NG):
            kb.op("dve", "memset", Sp[:], 0.0, writes=[bSp])
            for c in range(NCORES - 1):
                k = n % 2
                n += 1
                kb.dma("sp", dL[k], L[k][:], dr["Sall"][c][g], writes=[bL[k]])
                kb.op("dve", "tensor_tensor", out=tmp[:].rearrange("p (r c) -> p r c", r=8),
                      in0=Sp[:].rearrange("p (r c) -> p r c", r=8),
                      in1=dd[:, c, g * 8:g * 8 + 8].unsqueeze(2).to_broadcast([128, 8, 64]), op=ALU.mult,
                      reads=[bSp, bdd], writes=[btmp])
                kb.op("dve", "scalar_tensor_tensor", out=Sp[:], in0=L[k][:], scalar=ws[:, c:c + 1], in1=tmp[:],
                      op0=ALU.mult, op1=ALU.add, reads=[bL[k], bws, btmp], writes=[bSp])
            kb.dma("sp", d, dr["Sinit"][g], Sp[:], reads=[bSp])
        kb.barrier()


def phase_proj_res(kb, C, act_dram, kcin, w_dram, res_dram, res_off, gidx, out_dram):
    with ExitStack() as es:
        S = alloc_norm_scratch(kb, es)
        act = kb.sbuf("pr_act", [128, kcin, TT], BF16, es=es)
        res = kb.sbuf("pr_res", [128, KC, TT], F32, es=es)
        mix = kb.sbuf("pr_mix", [128, KC, TT], F32, es=es)
        bact, bres, bmix = Buf("act"), Buf("res"), Buf("mix")
        ring = WRing(kb, es, "pr_w", 3, kcin, 128)
        ps = [kb.psum(f"pr_ps{i}", [128, TT], es=es) for i in range(2)]
        bps = [PB("ps0"), PB("ps1")]
        d = kb.new_dma_sem("dpr")
        av = act_dram.rearrange("(kc p) t -> p kc t", p=128)
        rv = res_dram.rearrange("(kc p) t -> p kc t", p=128)
        ov = out_dram.rearrange("(kc p) t -> p kc t", p=128)
        pi = 0
        for tt in range(NTT):
            kb.dma("sp", d, act[:], av[:, :, tt * TT:(tt + 1) * TT], writes=[bact])
            kb.dma("sp", d, res[:], rv[:, :, res_off + tt * TT:res_off + (tt + 1) * TT], writes=[bres])

            def evac(j, p_, bp_):
                if j % 2 == 0:
                    kb.op("act", "copy", out=mix[:, j, :], in_=p_[:, :], reads=[bp_], writes=[bmix])
                else:
                    kb.op("dve", "tensor_copy", out=mix[:, j, :], in_=p_[:, :], reads=[bp_], writes=[bmix])
            pi = gemm_fm(kb, ring, ps, bps, act, bact, kcin, TT, w_dram, 0, D, 128, evac, pi0=pi)
            post_norm_residual(kb, C, S, mix, bmix, res, bres, TT, gidx)
            kb.dma("sp", d, ov[:, :, tt * TT:(tt + 1) * TT], res[:], reads=[bres])
        kb.barrier()


def phase_ffn(kb, C, h_dram, layer, wg, wu, wd, out_dram):
    with ExitStack() as es:
        S = alloc_norm_scratch(kb, es)
        res = kb.sbuf("ff_res", [128, KC, TT], F32, es=es)
        mix = kb.sbuf("ff_mix", [128, KC, TT], F32, es=es)
        u2 = kb.sbuf("ff_u2", [128, KC, TT], BF16, es=es)
        a = kb.sbuf("ff_a", [128, FU, TT], BF16, es=es)
        sg = [kb.sbuf(f"ff_sg{i}", [128, TT], F32, es=es) for i in range(2)]
        bres, bmix, bu2, ba = Buf("res"), Buf("mix"), Buf("u2"), Buf("a")
        bsg = [Buf("sg0"), Buf("sg1")]
        ring = WRing(kb, es, "ff_w", 3, FU, 128)
        ringgu = WRing(kb, es, "ff_gu", 6, KC, 128)
        psG = [kb.psum(f"ff_psG{i}", [128, TT], es=es) for i in range(2)]
        psU = [kb.psum(f"ff_psU{i}", [128, TT], es=es) for i in range(2)]
        psD = [kb.psum(f"ff_psD{i}", [128, TT], es=es) for i in range(2)]
        bG, bU, bD = [PB(), PB()], [PB(), PB()], [PB(), PB()]
        d = kb.new_dma_sem("dff")
        hv_ = h_dram.rearrange("(kc p) t -> p kc t", p=128)
        ov = out_dram.rearrange("(kc p) t -> p kc t", p=128)
        pi = 0
        for tt in range(NTT):
            kb.dma("sp", d, res[:], hv_[:, :, tt * TT:(tt + 1) * TT], writes=[bres])
            norm_tile(kb, C, S, res, bres, TT, layer * 4 + 2, u2, bu2)
            for fu in range(FU):
                k = fu % 2
                wgt, wgb = ringgu.load(wg[fu], KC, 128)
                tok = None
                for kc in range(KC):
                    tok = kb.op("pe", "matmul", psG[k][:, :], wgt[:, kc, :], u2[:, kc, :], start=(kc == 0),
                                stop=(kc == KC - 1), reads=[wgb, bu2], writes=[bG[k]] if kc == 0 else [],
                                sig=(kc == KC - 1))
                kb.mark(tok, reads=[wgb, bu2], writes=[bG[k]])
                wut, wub = ringgu.load(wu[fu], KC, 128)
                for kc in range(KC):
                    tok = kb.op("pe", "matmul", psU[k][:, :], wut[:, kc, :], u2[:, kc, :], start=(kc == 0),
                                stop=(kc == KC - 1), reads=[wub, bu2], writes=[bU[k]] if kc == 0 else [],
                                sig=(kc == KC - 1))
                kb.mark(tok, reads=[wub, bu2], writes=[bU[k]])
                kb.op("act", "activation", out=sg[k][:], in_=psG[k][:, :], func=AF.Silu, reads=[bG[k]], writes=[bsg[k]])
                kb.op("dve", "tensor_tensor", out=a[:, fu, :], in0=sg[k][:], in1=psU[k][:, :], op=ALU.mult,
                      reads=[bsg[k], bU[k]], writes=[ba])

            def evac(j, p_, bp_):
                if j % 2 == 0:
                    kb.op("act", "copy", out=mix[:, j, :], in_=p_[:, :], reads=[bp_], writes=[bmix])
                else:
                    kb.op("dve", "tensor_copy", out=mix[:, j, :], in_=p_[:, :], reads=[bp_], writes=[bmix])
            pi = gemm_fm(kb, ring, psD, bD, a, ba, FU, TT, wd, 0, D, 128, evac, pi0=pi)
            post_norm_residual(kb, C, S, mix, bmix, res, bres, TT, layer * 4 + 3)
            kb.dma("sp", d, ov[:, :, tt * TT:(tt + 1) * TT], res[:], reads=[bres])
        kb.barrier()


AT = 512
SLOPES = [2.0 ** (-(h + 1) / 4.0) for h in range(NQH)]


def phase_att(kb, C, dr, h_dram, out_dram):
    with ExitStack() as es:
        S = alloc_norm_scratch(kb, es)
        res = kb.sbuf("at_res", [128, KC, AT], F32, es=es)
        rawA = kb.sbuf("at_rawA", [128, 2 * KC * AT], BF16, es=es)
        mix = rawA[:].bitcast(F32).rearrange("p (k t) -> p k t", k=KC)
        uq = rawA[:, 0:KC * AT].rearrange("p (k t) -> p k t", k=KC)
        ukv = rawA[:, KC * AT:2 * KC * AT].rearrange("p (k t) -> p k t", k=KC)
        qT = kb.sbuf("at_qT", [128, KC, AT], BF16, es=es)
        oT = S.sq
        KT2 = kb.sbuf("at_KT2", [128, NKV, 128 + T], BF16, es=es)
        vtm = kb.sbuf("at_vtm", [128, NB + 1, 256], BF16, es=es)
        wv = kb.sbuf("at_wv", [128, KC, 256], BF16, es=es)
        distm = kb.sbuf("at_dist", [128, 2, 256], F32, es=es)
        sinkb = kb.sbuf("at_sink", [128, NQH], F32, es=es)
        lg = [kb.sbuf(f"at_lg{i}", [128, 8, 256], F32, es=es) for i in range(2)]
        Pb = kb.sbuf("at_P", [128, 8, 256], BF16, es=es)
        PT = kb.sbuf("at_PT", [128, 16, 128], BF16, es=es)
        otm = kb.sbuf("at_otm", [128, 512], BF16, es=es)
        sts = [kb.sbuf(f"at_st{i}", [128, 16, 4], F32, es=es) for i in range(2)]
        bres, bA, bqT = Buf(), Buf(), Buf()
        bmix = buq = bukv = bA
        boT = S.bsq
        bKT = [Buf() for _ in range(NB + 1)]
        bV = [Buf() for _ in range(NB + 1)]
        bwv, bcst, bP, bPT, botm = Buf(), Buf(), Buf(), Buf(), Buf()
        blg = [Buf(), Buf()]
        bsts = [Buf(), Buf()]
        ring = WRing(kb, es, "at_w", 4, KC, 128)
        psA = [kb.psum(f"at_psA{i}", [128, 512], es=es) for i in range(2)]
        bpsA = [PB(), PB()]
        psL = [kb.psum(f"at_psL{i}", [128, 512], es=es) for i in range(2)]
        bpsL = [PB(), PB()]
        psPT = kb.psum("at_psPT", [128, 8, 128], BF16, es=es)
        bpsPT = PB()
        psO = kb.psum("at_psO", [128, 512], es=es)
        bpsO = PB()
        bpsOh = [bpsO, bpsO]
        psOT = kb.psum("at_psOT", [128, 4, 128], BF16, es=es)
        bpsOT = PB()
        d = kb.new_dma_sem("dat")
        kb.dma("sp", d, distm[:], dr["distm"], writes=[bcst])
        kb.dma("sp", d, sinkb[:], dr["sinks"], writes=[bcst])
        kb.dma("pool", d, wv[:], dr["w_v"], writes=[bwv])
        hv_ = h_dram.rearrange("(kc p) t -> p kc t", p=128)
        ov = out_dram.rearrange("(kc p) t -> p kc t", p=128)
        pi = 0

        def kv_for(N, blk0, nblk):
            nonlocal pi

            def evk(j, p_, bp_):
                for q in range(nblk):
                    kb.op("act", "copy", out=KT2[:, j, (blk0 + q) * 128:(blk0 + q + 1) * 128],
                          in_=p_[:, q * 128:(q + 1) * 128], reads=[bp_], writes=[bKT[blk0 + q]])
            pi = gemm_fm(kb, ring, psA, bpsA, ukv, bukv, KC, N, dr["wk_dup"], 0, 512, 128, evk, pi0=pi)
            for q in range(nblk):
                ps, bps = psA[pi % 2], bpsA[pi % 2]
                pi += 1
                tok = None
                for kc in range(KC):
                    tok = kb.op("pe", "matmul", ps[:, 0:256], ukv[:, kc, q * 128:(q + 1) * 128], wv[:, kc, :],
                                start=(kc == 0), stop=(kc == KC - 1), reads=[bukv, bwv],
                                writes=[bps] if kc == 0 else [], sig=(kc == KC - 1))
                kb.mark(tok, reads=[bukv, bwv], writes=[bps])
                kb.op("dve", "tensor_copy", out=vtm[:, blk0 + q, :], in_=ps[:, 0:256], reads=[bps], writes=[bV[blk0 + q]])

        kb.dma("sp", d, res[:, :, 0:128], hv_[:, :, 0:128], writes=[bres])
        norm_tile(kb, C, S, res, bres, 128, 8, ukv, bukv)
        kv_for(128, 0, 1)
        for tt in range(T // AT):
            kb.dma("sp", d, res[:], hv_[:, :, 128 + tt * AT:128 + (tt + 1) * AT], writes=[bres])
            norm_tile(kb, C, S, res, bres, AT, 4, uq, buq, gidx2=8, out_bf2=ukv, bout2=bukv)
            kv_for(AT, 1 + tt * (AT // 128), AT // 128)

            def evq(j, p_, bp_):
                kb.op("act", "activation", out=qT[:, j, :], in_=p_[:, :AT], func=AF.Copy, scale=0.125,
                      reads=[bp_], writes=[bqT])
            pi = gemm_fm(kb, ring, psA, bpsA, uq, buq, KC, AT, dr["w_q"], 0, D, 128, evq, pi0=pi)
            rounds = [(qb, kvg) for qb in range(AT // 128) for kvg in range(NKV)]

            def R_L(ri):
                qb, kvg = rounds[ri]
                k = ri % 2
                bi = tt * (AT // 128) + qb
                qs = slice(qb * 128, (qb + 1) * 128)
                dsel = 0 if bi == 0 else 1
                for h8 in range(8):
                    h = kvg * 8 + h8
                    unit, half = h // 2, h % 2
                    pl, bpl = psL[h8 % 2], bpsL[h8 % 2]
                    kb.op("pe", "matmul", pl[:, 0:256],
                          qT[half * 64:(half + 1) * 64, unit, qs],
                          KT2[half * 64:(half + 1) * 64, kvg, bi * 128:bi * 128 + 256], start=True, stop=True,
                          reads=[bqT, bKT[bi], bKT[bi + 1]], writes=[bpl])
                    kb.op("dve", "scalar_tensor_tensor", out=lg[k][:, h8, :], in0=distm[:, dsel, :],
                          scalar=-SLOPES[h], in1=pl[:, 0:256], op0=ALU.mult,
                          op1=ALU.add, reads=[bcst, bpl], writes=[blg[k]])
                hs = slice(kvg * 8, kvg * 8 + 8)
                st = sts[k]
                kb.op("dve", "tensor_reduce", out=st[:, 0:8, 0], in_=lg[k][:], axis=AX.X, op=ALU.max,
                      reads=[blg[k]], writes=[bsts[k]])
                kb.op("dve", "tensor_tensor", out=st[:, 0:8, 0], in0=st[:, 0:8, 0], in1=sinkb[:, hs], op=ALU.max,
                      reads=[bcst], writes=[bsts[k]])
                kb.op("dve", "tensor_scalar", out=st[:, 0:8, 1], in0=st[:, 0:8, 0], scalar1=-1.0, scalar2=None,
                      op0=ALU.mult, writes=[bsts[k]])
                kb.op("dve", "tensor_tensor", out=st[:, 8:16, 0], in0=sinkb[:, hs], in1=st[:, 0:8, 0],
                      op=ALU.subtract, reads=[bcst], writes=[bsts[k]])

            def R_E(ri):
                k = ri % 2
                st = sts[k]
                for h8 in range(8):
                    kb.op("act", "activation", out=Pb[:, h8, :], in_=lg[k][:, h8, :], func=AF.Exp,
                          bias=st[:, h8, 1:2], accum_out=st[:, h8, 2:3], reads=[blg[k], bsts[k]],
                          writes=[bP, bsts[k]])
                kb.op("act", "activation", out=st[:, 8:16, 1], in_=st[:, 8:16, 0], func=AF.Exp, writes=[bsts[k]])

            def R_T(ri):
                for hh in range(2):
                    for h4 in range(4):
                        for hf in range(2):
                            tok = kb.op("pe", "transpose", out=psPT[:, h4 * 2 + hf, :],
                                        in_=Pb[:, hh * 4 + h4, hf * 128:(hf + 1) * 128], identity=C.ident[:],
                                        reads=[bP, C.b], writes=[bpsPT] if (h4 == 0 and hf == 0) else [],
                                        sig=(h4 == 3 and hf == 1))
                    kb.mark(tok, reads=[bP], writes=[bpsPT])
                    if hh == 0:
                        kb.op("act", "copy", out=PT[:, 0:8, :], in_=psPT[:, :, :], reads=[bpsPT], writes=[bPT])
                    else:
                        kb.op("dve", "tensor_copy", out=PT[:, 8:16, :], in_=psPT[:, :, :], reads=[bpsPT],
                              writes=[bPT])

            def R_V(ri):
                qb, kvg = rounds[ri]
                k = ri % 2
                st = sts[k]
                bi = tt * (AT // 128) + qb
                qs = slice(qb * 128, (qb + 1) * 128)
                for h8 in range(8):
                    c0 = h8 * 64
                    for hf in range(2):
                        tok = kb.op("pe", "matmul", psO[:, c0:c0 + 64], PT[:, h8 * 2 + hf, :],
                                    vtm[:, bi + hf, kvg * 64:(kvg + 1) * 64], start=(hf == 0), stop=(hf == 1),
                                    reads=[bPT, bV[bi], bV[bi + 1]],
                                    writes=[bpsO] if (h8 == 0 and hf == 0) else [],
                                    sig=(h8 == 7 and hf == 1))
                kb.mark(tok, reads=[bPT], writes=[bpsO])
                kb.op("dve", "tensor_tensor", out=st[:, 8:16, 2], in0=st[:, 8:16, 1], in1=st[:, 0:8, 2], op=ALU.add,
                      writes=[bsts[k]])
                kb.op("dve", "reciprocal", out=st[:, 8:16, 3], in_=st[:, 8:16, 2], writes=[bsts[k]])
                kb.op("dve", "tensor_tensor", out=otm[:, :].rearrange("p (r c) -> p r c", r=8),
                      in0=psO[:, :].rearrange("p (r c) -> p r c", r=8),
                      in1=st[:, 8:16, 3:4].to_broadcast([128, 8, 64]), op=ALU.mult,
                      reads=[bsts[k], bpsO], writes=[botm])
                for u4 in range(4):
                    tok = kb.op("pe", "transpose", out=psOT[:, u4, :], in_=otm[:, u4 * 128:(u4 + 1) * 128],
                                identity=C.ident[:], reads=[botm, C.b], writes=[bpsOT] if u4 == 0 else [],
                                sig=(u4 == 3))
                kb.mark(tok, reads=[botm], writes=[bpsOT])
                kb.op("act", "copy", out=oT[:, kvg * 4:(kvg + 1) * 4, qs], in_=psOT[:, :, :], reads=[bpsOT],
                      writes=[boT])

            R_L(0)
            for ri in range(len(rounds)):
                R_E(ri)
                if ri + 1 < len(rounds):
                    R_L(ri + 1)
                R_T(ri)
                R_V(ri)

            def evo(j, p_, bp_):
                if j % 2 == 0:
                    kb.op("act", "copy", out=mix[:, j, :], in_=p_[:, :AT], reads=[bp_], writes=[bmix])
                else:
                    kb.op("dve", "tensor_copy", out=mix[:, j, :], in_=p_[:, :AT], reads=[bp_], writes=[bmix])
            pi = gemm_fm(kb, ring, psA, bpsA, oT, boT, KC, AT, dr["w_o"], 0, D, 128, evo, pi0=pi)
            post_norm_residual(kb, C, S, mix, bmix, res, bres, AT, 5)
            kb.dma("sp", d, ov[:, :, tt * AT:(tt + 1) * AT], res[:], reads=[bres])
        kb.barrier()


CONST_SPECS = [("ident", [128, 128], BF16), ("ones_bf", [128, 128], BF16), ("maskT", [128, 128], BF16),
               ("ones_f", [128, 128], F32), ("U_f", [128, 128], F32), ("SL_f", [128, 128], F32),
               ("ncols", [128, 9, 16], F32)]


def _decl(nc, dr, name, shape, dt, kind):
    dr[name] = nc.dram_tensor(name, list(shape), dt, kind=kind).ap()


def _finish(kb):
    kb.barrier()


def _consts(norm_w, kv_norm_w):
    bf = ml_dtypes.bfloat16
    i = np.arange(128)
    c = {}
    c["ident"] = np.eye(128, dtype=np.float32).astype(bf)
    c["ones_bf"] = np.ones((128, 128), np.float32).astype(bf)
    c["maskT"] = (i[:, None] <= i[None, :]).astype(np.float32).astype(bf)
    c["ones_f"] = np.ones((128, 128), np.float32)
    c["U_f"] = (i[:, None] <= i[None, :]).astype(np.float32)
    c["SL_f"] = (i[:, None] > i[None, :]).astype(np.float32)
    g = np.concatenate([norm_w.reshape(8, D), kv_norm_w.reshape(1, D)], axis=0)
    c["ncols"] = np.ascontiguousarray(g.reshape(9, KC, 128).transpose(2, 0, 1))
    return c


def _bc(v, n=128):
    return np.ascontiguousarray(np.broadcast_to(np.asarray(v, np.float32)[None], (n,) + tuple(np.shape(v))))


_CACHE = {}


def _prog(name, fn):
    if name not in _CACHE:
        _CACHE[name] = fn()
    return _CACHE[name]


def all_gather8(kb, nc, cc, src, mid, dst, deps):
    kb._wait("pool", deps)
    g4 = [[0, 1, 2, 3], [4, 5, 6, 7]]
    g2 = [[i, i + 4] for i in range(4)]
    inst = nc.gpsimd.collective_compute("AllGather", ALU.bypass, replica_groups=g4, ins=[src.opt()], outs=[mid.opt()])
    cc[1] += 1
    inst.then_inc(cc[0], 1)
    nc.gpsimd.wait_ge(cc[0], cc[1])
    inst = nc.gpsimd.collective_compute("AllGather", ALU.bypass, replica_groups=g2, ins=[mid.opt()], outs=[dst.opt()])
    cc[1] += 1
    inst.then_inc(cc[0], 1)
    nc.gpsimd.wait_ge(cc[0], cc[1])
    return (cc[0], cc[1])


def halo_select(kb, C, dr, h8, h2h):
    with ExitStack() as es:
        hs = kb.sbuf("hs_sel", [128, 8], F32, es=es)
        acc = kb.sbuf("hs_acc", [128, KC, 128], F32, es=es)
        t = [kb.sbuf(f"hs_t{i}", [128, KC, 128], F32, es=es) for i in range(2)]
        bt = [Buf(), Buf()]
        dt_ = [kb.new_dma_sem("dhs0"), kb.new_dma_sem("dhs1")]
        bhs, bacc = Buf(), Buf()
        d = kb.new_dma_sem("dhs")
        kb.dma("sp", d, hs[:], dr["hsel"], writes=[bhs])
        kb.op("dve", "memset", acc[:], 0.0, writes=[bacc])
        for j in range(NCORES):
            k = j % 2
            for q in range(2):
                kb.dma("sp", dt_[k], t[k][:, q * 8:(q + 1) * 8, :],
                       h8[q][j * 1024:(j + 1) * 1024, :].rearrange("(kc p) t -> p kc t", p=128), writes=[bt[k]])
            kb.op("dve", "scalar_tensor_tensor", out=acc[:], in0=t[k][:], scalar=hs[:, j:j + 1], in1=acc[:],
                  op0=ALU.mult, op1=ALU.add, reads=[bt[k], bhs], writes=[bacc])
        kb.dma("sp", d, h2h[:, 0:128].rearrange("(kc p) t -> p kc t", p=128), acc[:], reads=[bacc])
        kb.barrier()


def build_fused():
    nc = bass.Bass("TRN2", target_bir_lowering=False)
    dr = {}
    for n, sh, dt in CONST_SPECS:
        _decl(nc, dr, n, sh, dt, "ExternalInput")
    ext = [("xT", [D, CH + T]), ("hv", [128, 3, 64]), ("w_in_u", [NG, 6, 128, KC, 128]),
           ("w_in_z", [NG, 128, KC, 512]), ("w_in_dt", [128, KC, 64]), ("conv_w", [128, 48, 4]),
           ("conv_b", [128, 48]), ("ssm_nw", [128, 32]), ("wsel", [128, 8]), ("hsel", [128, 8]),
           ("w_out", [16, 128, DI // 128, 128]), ("wg0", [FU, 128, KC, 128]), ("wu0", [FU, 128, KC, 128]),
           ("wd0", [16, 128, FU, 128]), ("w_q", [16, 128, KC, 128]), ("w_o", [16, 128, KC, 128]),
           ("wk_dup", [4, 128, KC, 128]), ("w_v", [128, KC, 256]),
           ("distm", [128, 2, 256]), ("sinks", [128, NQH]), ("wg1", [FU, 128, KC, 128]), ("wu1", [FU, 128, KC, 128]),
           ("wd1", [16, 128, FU, 128])]
    for n, sh in ext:
        _decl(nc, dr, n, sh, F32, "ExternalInput")
    _decl(nc, dr, "outT", [D, T], F32, "ExternalOutput")
    pkg = [nc.dram_tensor(f"pk_{g}", [256, 512], F32).ap() for g in range(NG // 2)]
    pkg4 = [nc.dram_tensor(f"pk4_{g}", [4 * 256, 512], F32).ap() for g in range(NG // 2)]
    pkg8 = [nc.dram_tensor(f"pk8_{g}", [8 * 256, 512], F32).ap() for g in range(NG // 2)]
    pkd = nc.dram_tensor("pkd", [256, 512], F32).ap()
    pkd4 = nc.dram_tensor("pkd4", [4 * 256, 512], F32).ap()
    pkd8 = nc.dram_tensor("pkd8", [8 * 256, 512], F32).ap()
    hsl = [nc.dram_tensor(f"hsl_{q}", [1024, 128], F32).ap() for q in range(2)]
    hsl4 = [nc.dram_tensor(f"hsl4_{q}", [4 * 1024, 128], F32).ap() for q in range(2)]
    hsl8 = [nc.dram_tensor(f"hsl8_{q}", [8 * 1024, 128], F32).ap() for q in range(2)]
    _decl(nc, dr, "Sinit", [NG, 128, 512], F32, "Internal")
    _decl(nc, dr, "ynT", [DI, T], BF16, "Internal")
    _decl(nc, dr, "h1T", [D, T], F32, "Internal")
    _decl(nc, dr, "h2h", [D, 128 + T], F32, "Internal")
    _decl(nc, dr, "h3T", [D, T], F32, "Internal")
    _decl(nc, dr, "xcc", [NG, 5, 128, T], BF16, "Internal")

    def dAview(ap_rows):
        return ap_rows.rearrange("(p two) f -> p (two f)", two=2).rearrange("p (b h) -> p b h", b=NB)
    dr["Sloc"] = [pkg[g // 2][(g % 2) * 128:(g % 2 + 1) * 128, :] for g in range(NG)]
    dr["dAblk"] = dAview(pkd)
    dr["Sall"] = [[pkg8[g // 2][c * 256 + (g % 2) * 128:c * 256 + (g % 2 + 1) * 128, :] for g in range(NG)]
                  for c in range(NCORES)]
    dr["dAall"] = [dAview(pkd8[c * 256:(c + 1) * 256, :]) for c in range(NCORES)]
    with ExitStack() as es:
        kb = KB(nc, es)
        kb.nsem += 1
        cc = [es.enter_context(nc.semaphore("ccsem")), 0]
        C = load_consts(kb, dr)
        ssm_pass(kb, C, dr, full=False)
        kb.barrier()
        for g in range(NG // 2):
            tok = all_gather8(kb, nc, cc, pkg[g], pkg4[g], pkg8[g], [])
        tok = all_gather8(kb, nc, cc, pkd, pkd4, pkd8, [])
        kb.barrier(extra=[tok])
        prefix_state(kb, C, dr)
        ssm_pass(kb, C, dr, full=True)
        phase_proj_res(kb, C, dr["ynT"], DI // 128, dr["w_out"], dr["xT"], CH, 1, dr["h1T"])
        phase_ffn(kb, C, dr["h1T"], 0, dr["wg0"], dr["wu0"], dr["wd0"], dr["h2h"][:, 128:128 + T])
        dh = kb.new_dma_sem("dhsl")
        for q in range(2):
            kb.dma("sp", dh, hsl[q], dr["h2h"][q * 1024:(q + 1) * 1024, T:T + 128])
        kb.barrier()
        for q in range(2):
            tok = all_gather8(kb, nc, cc, hsl[q], hsl4[q], hsl8[q], [])
        kb.barrier(extra=[tok])
        halo_select(kb, C, dr, hsl8, dr["h2h"])
        phase_att(kb, C, dr, dr["h2h"], dr["h3T"])
        phase_ffn(kb, C, dr["h3T"], 1, dr["wg1"], dr["wu1"], dr["wd1"], dr["outT"])
        _finish(kb)
    return nc


def kernel(x, norm_w, ssm_w_in, ssm_conv_w, ssm_conv_b, ssm_dt_bias, ssm_A_log, ssm_D, ssm_norm_w, ssm_w_out,
           kv_norm_w, w_kv, attn_w_q, attn_sinks, attn_w_o, ffn_w_gate, ffn_w_up, ffn_w_down):
    f32 = np.float32
    x = np.asarray(x, f32)
    cores = list(range(NCORES))
    base = _consts(np.asarray(norm_w, f32), np.asarray(kv_norm_w, f32))
    xpad = np.concatenate([np.zeros((CH, D), f32), x[0]], axis=0)
    hv = np.stack([_bc(np.asarray(ssm_dt_bias, f32)[0]), _bc(np.asarray(ssm_A_log, f32)[0]),
                   _bc(np.asarray(ssm_D, f32)[0])], axis=1)
    wg = np.asarray(ffn_w_gate, f32)
    wu = np.asarray(ffn_w_up, f32)
    wd = np.asarray(ffn_w_down, f32)
    wkv = np.asarray(w_kv, f32)
    wk = wkv[:, :256].reshape(D, NKV, 1, HD)
    qi = np.arange(128)[:, None]
    sj = np.arange(256)[None, :]
    dist = (128 + qi - sj).astype(f32)
    BIG = 1.0e6
    dm = np.where((dist >= 0) & (dist < WIN), dist, BIG).astype(f32)
    dm_first = dm.copy()
    dm_first[:, :128] = BIG
    def tw(w, uw):
        K_, N_ = w.shape
        return np.ascontiguousarray(w.reshape(K_ // 128, 128, N_ // uw, uw).transpose(2, 1, 0, 3))
    w_in = np.asarray(ssm_w_in, f32)[0]
    w_in_x = tw(w_in[:, C_X:C_X + DI], 128).reshape(NG, 4, 128, KC, 128)
    w_in_B = tw(w_in[:, C_B:C_B + 1024], 128).reshape(NG, 1, 128, KC, 128)
    w_in_C = tw(w_in[:, C_C:C_C + 1024], 128).reshape(NG, 1, 128, KC, 128)
    shared = dict(
        base, hv=hv, w_in_u=np.ascontiguousarray(np.concatenate([w_in_x, w_in_B, w_in_C], axis=1)),
        w_in_z=tw(w_in[:, C_Z:C_Z + DI], 512), w_in_dt=tw(w_in[:, C_DT:C_DT + 64], 64)[0],
        conv_w=np.ascontiguousarray(np.asarray(ssm_conv_w, f32)[0].reshape(4, 48, 128).transpose(2, 1, 0)),
        conv_b=np.ascontiguousarray(np.asarray(ssm_conv_b, f32)[0].reshape(48, 128).T),
        ssm_nw=np.ascontiguousarray(np.asarray(ssm_norm_w, f32)[0].reshape(32, 128).T),
        w_out=tw(np.asarray(ssm_w_out, f32)[0], 128),
        wg0=tw(wg[0], 128), wu0=tw(wu[0], 128), wd0=tw(wd[0], 128),
        wg1=tw(wg[1], 128), wu1=tw(wu[1], 128), wd1=tw(wd[1], 128),
        w_q=tw(np.asarray(attn_w_q, f32)[0], 128), w_o=tw(np.asarray(attn_w_o, f32)[0], 128),
        wk_dup=tw(np.ascontiguousarray(np.broadcast_to(wk, (D, NKV, 2, HD)).reshape(D, 512)), 128),
        w_v=tw(np.ascontiguousarray(wkv[:, 256:]), 256)[0], sinks=_bc(np.asarray(attn_sinks, f32)[0]))
    in_maps = []
    for c in cores:
        in_maps.append(dict(
            shared, xT=np.ascontiguousarray(xpad[c * T:c * T + CH + T].T),
            wsel=_bc((np.arange(8) < c).astype(f32)), hsel=_bc((np.arange(8) == c - 1).astype(f32)),
            distm=np.ascontiguousarray(np.stack([dm_first if c == 0 else dm, dm], axis=1))))
    nc = _prog("fused", build_fused)
    res = run_bass_kernel_spmd(nc, in_maps, core_ids=cores).results
    out = np.concatenate([res[c]["outT"].T for c in cores], axis=0)
    return np.ascontiguousarray(out[None]).astype(f32)
```

```python
from contextlib import ExitStack
import os
import numpy as np
import ml_dtypes
import concourse.bass as bass
import concourse.mybir as mybir
from concourse.bass_utils import run_bass_kernel_spmd

F32 = mybir.dt.float32
F32R = mybir.dt.float32r
BF16 = mybir.dt.bfloat16
AF = mybir.ActivationFunctionType
ALU = mybir.AluOpType
AX = mybir.AxisListType

NCORES = 8
D = 2048
KC = 16
T = 2048
NB = T // 128
TT = 512
NTT = T // TT
CH = 4
DI = 4096
NG = 8
NH = 64
DFF = 5632
FU = DFF // 128
EPS = 1e-6
WIN = 128
NQH = 32
NKV = 4
HD = 64
C_Z, C_X, C_B, C_C, C_DT = 0, 4096, 8192, 9216, 10240
NIN = 10304


class Buf:
    __slots__ = ("name", "w", "r", "excl")

    def __init__(self, name="", excl=False):
        self.name = name
        self.w = None
        self.r = []
        self.excl = excl


def PB(name=""):
    return Buf(name, excl=True)


class KB:
    SEM_LIMIT = 30000

    def __init__(self, nc, es):
        self.nc = nc
        self.es = es
        self.engs = {"pe": nc.tensor, "act": nc.scalar, "dve": nc.vector, "pool": nc.gpsimd, "sp": nc.sync}
        self.sem, self.cnt, self.waited = {}, {}, {}
        self.nsem = 0
        self.dsems = []
        for e in self.engs:
            self._new_sem(e)

    def _new_sem(self, e):
        self.nsem += 1
        self.sem[e] = self.es.enter_context(self.nc.semaphore(f"s{self.nsem}_{e}"))
        self.cnt[e] = 0

    def new_dma_sem(self, name="d"):
        self.nsem += 1
        d = [self.es.enter_context(self.nc.semaphore(f"s{self.nsem}_{name}")), 0]
        self.dsems.append(d)
        return d

    def sbuf(self, name, shape, dtype, es=None):
        self.nsem += 1
        return (es or self.es).enter_context(self.nc.sbuf_tensor(f"sb{self.nsem}_{name}", list(shape), dtype))

    def psum(self, name, shape, dtype=F32, es=None):
        self.nsem += 1
        n = 1
        for d_ in shape[1:]:
            n *= d_
        full = 512 if dtype == F32 else 1024
        assert n <= full
        t = (es or self.es).enter_context(self.nc.psum_tensor(f"ps{self.nsem}_{name}", [128, full], dtype))
        v = t[:, 0:n]
        if len(shape) == 3:
            v = v.rearrange("p (a b) -> p a b", a=shape[1])
        return v

    def _wait(self, e, toks):
        best = {}
        for t in toks:
            if t is None:
                continue
            s, v = t
            k = id(s)
            if k not in best or best[k][1] < v:
                best[k] = (s, v)
        for k, (s, v) in best.items():
            if self.waited.get((e, k), 0) >= v:
                continue
            self.engs[e].wait_ge(s, v)
            self.waited[(e, k)] = v

    @staticmethod
    def _deps(reads, writes, extra):
        toks = list(extra)
        for b in reads:
            toks.append(b.w)
            if b.excl:
                toks.extend(b.r)
        for b in writes:
            toks.append(b.w)
            toks.extend(b.r)
        return toks

    @staticmethod
    def _retire(tok, reads, writes):
        for b in reads:
            b.r.append(tok)
            if len(b.r) > 48:
                b.r = b.r[-48:]
        for b in writes:
            b.w = tok
            b.r = []

    def op(self, e, meth, *args, reads=(), writes=(), deps=(), sig=True, **kw):
        self._wait(e, self._deps(reads, writes, deps))
        inst = getattr(self.engs[e], meth)(*args, **kw)
        tok = None
        if sig:
            if self.cnt[e] >= self.SEM_LIMIT:
                self._new_sem(e)
            self.cnt[e] += 1
            inst.then_inc(self.sem[e], 1)
            tok = (self.sem[e], self.cnt[e])
            self._retire(tok, reads, writes)
        return tok

    def mark(self, tok, reads=(), writes=()):
        self._retire(tok, reads, writes)

    def dma(self, e, dsem, out, in_, reads=(), writes=(), deps=(), **kw):
        self._wait(e, self._deps(reads, writes, deps))
        if dsem[1] >= self.SEM_LIMIT:
            self.nsem += 1
            dsem[0] = self.es.enter_context(self.nc.semaphore(f"s{self.nsem}_dx"))
            dsem[1] = 0
        inst = self.engs[e].dma_start(out=out, in_=in_, **kw)
        dsem[1] += 16
        inst.then_inc(dsem[0], 16)
        tok = (dsem[0], dsem[1])
        self._retire(tok, reads, writes)
        return tok

    def cur(self, e):
        return (self.sem[e], self.cnt[e]) if self.cnt[e] > 0 else None

    def barrier(self, extra=()):
        toks = [self.cur(e) for e in self.engs] + [(d[0], d[1]) for d in self.dsems if d[1] > 0] + list(extra)
        for e in self.engs:
            self._wait(e, toks)


class NS:
    pass


def load_consts(kb, dr):
    C = NS()
    d = kb.new_dma_sem("dc")
    C.b = Buf("consts")
    specs = [("ident", [128, 128], BF16), ("ones_bf", [128, 128], BF16), ("maskT", [128, 128], BF16),
             ("ones_f", [128, 128], F32), ("U_f", [128, 128], F32), ("SL_f", [128, 128], F32),
             ("ncols", [128, 9, 16], F32)]
    toks = []
    for name, shape, dt in specs:
        t = kb.sbuf("c_" + name, shape, dt)
        setattr(C, name, t)
        toks.append(kb.dma("sp", d, t[:], dr[name]))
    C.b.w = toks[-1]
    C.b.w = (d[0], d[1])
    return C


def rms_rstd(kb, C, ps, bps, rstd, brstd, N, nfeat):
    kb.op("dve", "tensor_scalar", out=rstd[:, :N], in0=ps[:, :N], scalar1=1.0 / nfeat, scalar2=EPS,
          op0=ALU.mult, op1=ALU.add, reads=[bps], writes=[brstd])
    kb.op("act", "activation", out=rstd[:, :N], in_=rstd[:, :N], func=AF.Sqrt, reads=[brstd], writes=[brstd])
    kb.op("dve", "reciprocal", out=rstd[:, :N], in_=rstd[:, :N], reads=[brstd], writes=[brstd])


def sumsq_fm(kb, C, sq, bsq, ps, bps, N, nkc=KC):
    tok = None
    for kc in range(nkc):
        tok = kb.op("pe", "matmul", ps[:, :N], C.ones_bf[:], sq[:, kc, :N], start=(kc == 0), stop=(kc == nkc - 1),
                    reads=[bsq, C.b], writes=[bps] if kc == 0 else [], sig=(kc == nkc - 1))
    kb.mark(tok, reads=[bsq], writes=[bps])


class WRing:
    def __init__(self, kb, es, name, nbuf, kcin, width):
        self.kb = kb
        self.n = nbuf
        self.t = [kb.sbuf(f"{name}{i}", [128, kcin, width], BF16, es=es) for i in range(nbuf)]
        self.b = [Buf(f"{name}{i}") for i in range(nbuf)]
        self.d = [kb.new_dma_sem(f"{name}{i}") for i in range(nbuf)]
        self.i = 0

    def load(self, w_ap, kcin, width, eng="pool"):
        i = self.i % self.n
        self.i += 1
        self.kb.dma(eng, self.d[i], self.t[i][:, :kcin, :width], w_ap, writes=[self.b[i]])
        return self.t[i], self.b[i]


def norm_tile(kb, C, S, src, bsrc, N, gidx, out_bf, bout, out_off=0, gidx2=None, out_bf2=None, bout2=None):
    kb.op("act", "activation", out=S.sq[:, :, :N], in_=src[:, :, :N], func=AF.Square, reads=[bsrc], writes=[S.bsq])
    sumsq_fm(kb, C, S.sq, S.bsq, S.psn, S.bpsn, N)
    rms_rstd(kb, C, S.psn, S.bpsn, S.rstd, S.brstd, N, D)
    for kc in range(KC):
        kb.op("dve", "scalar_tensor_tensor", out=out_bf[:, kc, out_off:out_off + N], in0=src[:, kc, :N],
              scalar=C.ncols[:, gidx, kc:kc + 1], in1=S.rstd[:, :N], op0=ALU.mult, op1=ALU.mult,
              reads=[bsrc, S.brstd, C.b], writes=[bout])
        if gidx2 is not None:
            kb.op("dve", "scalar_tensor_tensor", out=out_bf2[:, kc, out_off:out_off + N], in0=src[:, kc, :N],
                  scalar=C.ncols[:, gidx2, kc:kc + 1], in1=S.rstd[:, :N], op0=ALU.mult, op1=ALU.mult,
                  reads=[bsrc, S.brstd, C.b], writes=[bout2])


def post_norm_residual(kb, C, S, mix, bmix, res, bres, N, gidx):
    kb.op("act", "activation", out=S.sq[:, :, :N], in_=mix[:, :, :N], func=AF.Square, reads=[bmix], writes=[S.bsq])
    sumsq_fm(kb, C, S.sq, S.bsq, S.psn, S.bpsn, N)
    rms_rstd(kb, C, S.psn, S.bpsn, S.rstd, S.brstd, N, D)
    for kc in range(KC):
        e = "dve"
        kb.op(e, "scalar_tensor_tensor", out=mix[:, kc, :N], in0=mix[:, kc, :N],
              scalar=C.ncols[:, gidx, kc:kc + 1], in1=S.rstd[:, :N], op0=ALU.mult, op1=ALU.mult,
              reads=[S.brstd, C.b], writes=[bmix])
    kb.op("dve", "tensor_tensor", out=res[:, :, :N], in0=res[:, :, :N], in1=mix[:, :, :N], op=ALU.add,
          reads=[bmix], writes=[bres])


def alloc_norm_scratch(kb, es):
    S = NS()
    S.sq = kb.sbuf("n_sq", [128, KC, TT], BF16, es=es)
    S.bsq = Buf("sq")
    S.psn = kb.psum("n_ps", [128, TT], es=es)
    S.bpsn = PB("psn")
    S.rstd = kb.sbuf("n_rstd", [128, TT], F32, es=es)
    S.brstd = Buf("rstd")
    return S


def gemm_fm(kb, ring, psums, bpsums, act, bact, kcin, N, w_dram, col0, ncols, uw, evac, pi0=0):
    nunit = ncols // uw
    pi = pi0
    for u in range(nunit):
        wt, wb = ring.load(w_dram[u], kcin, uw)
        for s in range(uw // 128):
            ps, bps = psums[pi % len(psums)], bpsums[pi % len(psums)]
            pi += 1
            tok = None
            for kc in range(kcin):
                tok = kb.op("pe", "matmul", ps[:, :N], wt[:, kc, s * 128:(s + 1) * 128], act[:, kc, :N],
                            start=(kc == 0), stop=(kc == kcin - 1), reads=[wb, bact],
                            writes=[bps] if kc == 0 else [], sig=(kc == kcin - 1))
            kb.mark(tok, reads=[wb, bact], writes=[bps])
            evac(u * (uw // 128) + s, ps, bps)
    return pi


def ssm_pass(kb, C, dr, full, dbg=None):
    nc = kb.nc
    with ExitStack() as es:
        u0 = kb.sbuf("u0", [128, KC, CH + T], BF16, es=es)
        bu0 = [Buf(f"u0_{i}") for i in range(NTT + 1)]
        hv = kb.sbuf("hv", [128, 5, 64], F32, es=es)
        bhv = Buf("hv")
        dtm = kb.sbuf("dtm", [128, NB, 64], F32, es=es)
        atm = kb.sbuf("atm", [128, NB, 64], F32, es=es)
        einm = kb.sbuf("einm", [128, NB, 64], F32, es=es)
        w2m = kb.sbuf("w2m", [128, NB, 64], F32, es=es)
        dAm = kb.sbuf("dAm", [128, NB, 64], F32, es=es)
        bpre = [Buf(f"pre{b}") for b in range(NB)]
        dsm = kb.new_dma_sem("dsm")
        kb.dma("sp", dsm, hv[:, 0:3, :], dr["hv"], writes=[bhv])
        kb.op("act", "activation", out=hv[:, 3, :], in_=hv[:, 1, :], func=AF.Exp, reads=[bhv], writes=[bhv])
        kb.op("dve", "tensor_scalar", out=hv[:, 1, :], in0=hv[:, 3, :], scalar1=-1.0, scalar2=None, op0=ALU.mult,
              reads=[bhv], writes=[bhv])

        pre_arrs = [dtm, atm, einm, w2m, dAm]
        if full:
            kb.dma("sp", dsm, u0[:], dr["u0c"], writes=bu0)
            for i_, arr in enumerate(pre_arrs):
                kb.dma("sp", dsm, arr[:], dr["prec"][i_], writes=bpre)
            kb.barrier()
        if not full:
            with ExitStack() as es1:
                S = alloc_norm_scratch(kb, es1)
                xs = [kb.sbuf(f"xs{i}", [128, KC, TT], F32, es=es1) for i in range(2)]
                bxs = [Buf("xs0"), Buf("xs1")]
                dxs = [kb.new_dma_sem("dxs0"), kb.new_dma_sem("dxs1")]
                xTv = dr["xT"].rearrange("(kc p) t -> p kc t", p=128)
                order = [(-1, 0, CH)] + [(i, CH + i * TT, TT) for i in range(NTT)]
                for n, (ti, c0, N) in enumerate(order):
                    k = n % 2
                    kb.dma("sp", dxs[k], xs[k][:, :, :N], xTv[:, :, c0:c0 + N], writes=[bxs[k]])
                    norm_tile(kb, C, S, xs[k], bxs[k], N, 0, u0, bu0[ti + 1], out_off=c0)
                kb.dma("sp", dsm, dr["u0c"], u0[:], reads=bu0)
                kb.barrier()
        if os.environ.get("MK_STOP") == "1":
            return

        if not full:
            with ExitStack() as es2:
                wdt = kb.sbuf("wdt", [128, KC, 64], BF16, es=es2)
                bwdt = Buf("wdt")
                kb.dma("pool", dsm, wdt[:], dr["w_in_dt"], writes=[bwdt])
                psd = [kb.psum(f"psd{i}", [128, 64], es=es2) for i in range(3)]
                bpsd = [PB(f"psd{i}") for i in range(3)]
                t1 = kb.sbuf("t1", [128, 64], F32, es=es2)
                t2 = kb.sbuf("t2", [128, 64], F32, es=es2)
                t3 = kb.sbuf("t3", [128, 64], F32, es=es2)
                bt = [Buf("t1"), Buf("t2"), Buf("t3")]
                for b in range(NB):
                    ti = b // 4
                    c0 = CH + b * 128
                    tok = None
                    for kc in range(KC):
                        tok = kb.op("pe", "matmul", psd[0][:, :], u0[:, kc, c0:c0 + 128], wdt[:, kc, :],
                                    start=(kc == 0), stop=(kc == KC - 1), reads=[bu0[ti + 1], bwdt],
                                    writes=[bpsd[0]] if kc == 0 else [], sig=(kc == KC - 1))
                    kb.mark(tok, reads=[bwdt], writes=[bpsd[0]])
                    kb.op("dve", "tensor_tensor", out=t1[:], in0=psd[0][:, :], in1=hv[:, 0, :], op=ALU.add,
                          reads=[bpsd[0], bhv], writes=[bt[0]])
                    kb.op("dve", "scalar_tensor_tensor", out=t2[:], in0=t1[:], scalar=-1.0, in1=t1[:], op0=ALU.mult,
                          op1=ALU.max, reads=[bt[0]], writes=[bt[1]])
                    kb.op("act", "activation", out=t2[:], in_=t2[:], func=AF.Exp, scale=-1.0, reads=[bt[1]], writes=[bt[1]])
                    kb.op("act", "activation", out=t2[:], in_=t2[:], func=AF.Ln, bias=1.0, reads=[bt[1]], writes=[bt[1]])
                    kb.op("dve", "scalar_tensor_tensor", out=dtm[:, b, :], in0=t1[:], scalar=0.0, in1=t2[:], op0=ALU.max,
                          op1=ALU.add, reads=[bt[0], bt[1]], writes=[bpre[b]])
                    kb.op("dve", "tensor_tensor", out=atm[:, b, :], in0=dtm[:, b, :], in1=hv[:, 1, :], op=ALU.mult,
                          reads=[bpre[b], bhv], writes=[bpre[b]])
                    kb.op("pe", "matmul", psd[1][:, :], C.U_f[:], atm[:, b, :], start=True, stop=True,
                          reads=[bpre[b], C.b], writes=[bpsd[1]])
                    kb.op("pe", "matmul", psd[2][:, :], C.ones_f[:], atm[:, b, :], start=True, stop=True,
                          reads=[bpre[b], C.b], writes=[bpsd[2]])
                    kb.op("act", "activation", out=einm[:, b, :], in_=psd[1][:, :], func=AF.Exp, reads=[bpsd[1]],
                          writes=[bpre[b]])
                    kb.op("act", "activation", out=dAm[:, b, :], in_=psd[2][:, :], func=AF.Exp, reads=[bpsd[2]],
                          writes=[bpre[b]])
                    kb.op("act", "activation", out=t3[:], in_=psd[1][:, :], func=AF.Identity, reads=[bpsd[1]], writes=[bt[2]])
                    kb.op("dve", "tensor_tensor", out=t3[:], in0=psd[2][:, :], in1=t3[:], op=ALU.subtract,
                          reads=[bpsd[2], bt[2]], writes=[bt[2]])
                    kb.op("act", "activation", out=t3[:], in_=t3[:], func=AF.Exp, reads=[bt[2]], writes=[bt[2]])
                    kb.op("dve", "tensor_tensor", out=w2m[:, b, :], in0=t3[:], in1=dtm[:, b, :], op=ALU.mult,
                          reads=[bt[2], bpre[b]], writes=[bpre[b]])
                for i_, arr in enumerate(pre_arrs):
                    kb.dma("sp", dsm, dr["prec"][i_], arr[:], reads=bpre)
                if not full:
                    kb.op("dve", "tensor_copy", out=t1[:], in_=dAm[:, NB - 1, :], reads=[bpre[NB - 1]], writes=[bt[0]])
                    for b in range(NB - 2, -1, -1):
                        kb.op("dve", "tensor_tensor", out=w2m[:, b, :], in0=w2m[:, b, :], in1=t1[:], op=ALU.mult,
                              reads=[bt[0]], writes=[bpre[b]])
                        if b > 0:
                            kb.op("dve", "tensor_tensor", out=t1[:], in0=t1[:], in1=dAm[:, b, :], op=ALU.mult,
                                  reads=[bpre[b]], writes=[bt[0]])
                kb.barrier()
        if os.environ.get("MK_STOP") == "2":
            return
        if dbg is not None and "dt" in dbg:
            kb.dma("sp", dsm, dbg["dt"], dtm[:], reads=bpre)
            kb.dma("sp", dsm, dbg["ein"], einm[:], reads=bpre)
            kb.dma("sp", dsm, dbg["w2"], w2m[:], reads=bpre)
            kb.dma("sp", dsm, dbg["dA"], dAm[:], reads=bpre)

        with ExitStack() as es3:
            units = ["x0", "x1", "x2", "x3", "B"] + (["C"] if full else [])
            nun = len(units)
            wu = {n: kb.sbuf(f"w_{n}", [128, KC, 128], BF16, es=es3) for n in units}
            bwu = {n: Buf(f"w_{n}") for n in units}
            dwu = {n: kb.new_dma_sem(f"dw_{n}") for n in units}
            if full:
                wz = kb.sbuf("w_z", [128, KC, 512], BF16, es=es3)
                bwz = Buf("w_z")
                dwz = kb.new_dma_sem("dw_z")
            cw = kb.sbuf("cw", [128, 48, 4], F32, es=es3)
            cbv = kb.sbuf("cbv", [128, 48], F32, es=es3)
            nwc = kb.sbuf("nwc", [128, 32], F32, es=es3)
            bcv = Buf("convw")
            kb.dma("sp", dsm, cw[:], dr["conv_w"], writes=[bcv])
            kb.dma("sp", dsm, cbv[:], dr["conv_b"], writes=[bcv])
            kb.dma("sp", dsm, nwc[:], dr["ssm_nw"], writes=[bcv])
            pre = kb.sbuf("pre", [128, nun, 3 + TT], F32, es=es3)
            bpreu = [Buf(f"preu{i}") for i in range(nun)]
            carry = kb.sbuf("carry", [128, nun, 3], F32, es=es3)
            bcarry = [Buf(f"carry{i}") for i in range(nun)]
            acc = [kb.sbuf(f"acc{i}", [128, TT], F32, es=es3) for i in range(2)]
            bacc = [Buf("acc0"), Buf("acc1")]
            xc = kb.sbuf("xc", [128, nun, TT], BF16, es=es3)
            bxc = [Buf(f"xc{i}") for i in range(nun)]
            St = kb.sbuf("St", [128, 512], F32, es=es3)
            Stb = kb.sbuf("Stb", [128, 512], BF16, es=es3)
            bSt, bStb = Buf("St"), Buf("Stb")
            Stmp = kb.sbuf("Stmp", [128, 512], F32, es=es3)
            bStmp = Buf("Stmp")
            xtm = [kb.sbuf(f"xtm{i}", [128, 512], BF16, es=es3) for i in range(2)]
            xw = [kb.sbuf(f"xw{i}", [128, 512], BF16, es=es3) for i in range(2)]
            btm = [kb.sbuf(f"btm{i}", [128, 128], BF16, es=es3) for i in range(2)]
            bxtm, bxw, bbtm = [Buf(), Buf()], [Buf(), Buf()], [Buf(), Buf()]
            psA = [kb.psum(f"psA{i}", [128, 512], es=es3) for i in range(2)]
            bpsA = [PB("psA0"), PB("psA1")]
            psT = kb.psum("psT", [128, 5, 128], BF16, es=es3)
            bpsT = PB("psT")
            psS = kb.psum("psS", [128, 512], es=es3)
            bpsS = PB("psS")
            psh, bpsh = psS, bpsS
            dout = kb.new_dma_sem("dout")
            dxcc = kb.new_dma_sem("dxcc")
            if full:
                xdt = [kb.sbuf(f"xdt{i}", [128, 512], BF16, es=es3) for i in range(2)]
                bxdt = [Buf(), Buf()]
                zs = kb.sbuf("zs", [128, 4, 512], BF16, es=es3)
                bzs = [Buf(f"zs{i}") for i in range(4)]
                cbm = kb.sbuf("cbm", [128, 128], BF16, es=es3)
                bcbm = Buf("cbm")
                Ua = kb.sbuf("Ua", [128, 8, 128], F32, es=es3)
                bUa = Buf("Ua")
                Ee = kb.sbuf("Ee", [128, 8, 128], BF16, es=es3)
                bEe = Buf("Ee")
                Mt = [kb.sbuf(f"Mt{i}", [128, 8, 128], BF16, es=es3) for i in range(2)]
                bMt = [Buf(), Buf()]
                yd = kb.sbuf("yd", [128, 512], F32, es=es3)
                byd = Buf("yd")
                yt = kb.sbuf("yt", [128, 512], F32, es=es3)
                byt = Buf("yt")
                ynb = kb.sbuf("ynb", [128, 512], BF16, es=es3)
                bynb = Buf("ynb")
                idD = kb.sbuf("idD", [128, 16, 128], BF16, es=es3)
                bidD = Buf("idD")
                dtmp = kb.sbuf("dtmp", [128, 16], F32, es=es3)
                bdtmp = Buf("dtmp")
                dhib = kb.sbuf("dhib", [128, 8], BF16, es=es3)
                psT2 = kb.psum("psT2", [128, 4, 128], BF16, es=es3)
                bpsT2 = PB("psT2")
                junk = kb.sbuf("junk", [128, 512], BF16, es=es3)
                bjunk = Buf("junk")
                st8 = kb.sbuf("st8", [128, 8], F32, es=es3)
                bst8 = Buf("st8")
                ynT = [kb.sbuf(f"ynT{i}", [128, 4, TT], BF16, es=es3) for i in range(2)]
                bynT = [Buf("ynT0"), Buf("ynT1")]
                psE = [kb.psum(f"psE{i}", [128, 512], es=es3) for i in range(2)]
                bpsE = [PB("psE0"), PB("psE1")]
                psY = kb.psum("psY", [128, 512], es=es3)
                bpsY = PB("psY")
            else:
                psE = None
                psAcc = kb.psum("psAcc", [128, 512], es=es3)
                bpsAcc = PB("psAcc")

            for g in range(NG):
                cols = {"x0": C_X + g * 512, "x1": C_X + g * 512 + 128, "x2": C_X + g * 512 + 256,
                        "x3": C_X + g * 512 + 384, "B": C_B + g * 128, "C": C_C + g * 128}
                for ui_, n in enumerate(units):
                    if full and ui_ < 5:
                        continue
                    kb.dma("pool", dwu[n], wu[n][:], dr["w_in_u"][g, ui_], writes=[bwu[n]])
                if full:
                    kb.dma("pool", dwz, wz[:], dr["w_in_z"][g], writes=[bwz])
                    kb.dma("sp", dsm, St[:], dr["Sinit"][g], writes=[bSt])
                    kb.op("act", "copy", out=Stb[:], in_=St[:], reads=[bSt], writes=[bStb])
                    kb.op("dve", "tensor_copy", out=dhib[:], in_=hv[:, 2, g * 8:g * 8 + 8], reads=[bhv], writes=[bdtmp])
                    kb.op("dve", "tensor_copy", out=dtmp[:, 0:8], in_=dhib[:], writes=[bdtmp])
                    kb.op("dve", "tensor_tensor", out=dtmp[:, 8:16], in0=hv[:, 2, g * 8:g * 8 + 8], in1=dtmp[:, 0:8],
                          op=ALU.subtract, reads=[bhv], writes=[bdtmp])
                    for r in range(16):
                        kb.op("act", "activation", out=idD[:, r, :], in_=C.ident[:], func=AF.Identity,
                              scale=dtmp[:, r:r + 1], reads=[C.b, bdtmp], writes=[bidD])
                cunit = [g * 4 + 0, g * 4 + 1, g * 4 + 2, g * 4 + 3, 32 + g, 40 + g]
                for tt in range(NTT):
                    c0 = CH + tt * TT
                    if full:
                        kb.dma("sp", dxcc, xc[:, 0:5, :],
                               dr["xcc"][g, :, :, tt * TT:(tt + 1) * TT].rearrange("u p t -> p u t"),
                               writes=[bxc[0], bxc[1], bxc[2], bxc[3], bxc[4]])
                    for ui, n in enumerate(units):
                        if full and ui < 5:
                            continue
                        if tt == 0:
                            tok = None
                            for kc in range(KC):
                                tok = kb.op("pe", "matmul", psh[:, 0:CH], wu[n][:, kc, :], u0[:, kc, 0:CH],
                                            start=(kc == 0), stop=(kc == KC - 1), reads=[bwu[n], bu0[0]],
                                            writes=[bpsh] if kc == 0 else [], sig=(kc == KC - 1))
                            kb.mark(tok, reads=[bwu[n]], writes=[bpsh])
                            kb.op("act", "copy", out=pre[:, ui, 0:3], in_=psh[:, 1:CH], reads=[bpsh],
                                  writes=[bpreu[ui]])
                        else:
                            kb.op("act", "copy", out=pre[:, ui, 0:3], in_=carry[:, ui, :], reads=[bcarry[ui]],
                                  writes=[bpreu[ui]])
                        ps, bps = psA[ui % 2], bpsA[ui % 2]
                        tok = None
                        for kc in range(KC):
                            tok = kb.op("pe", "matmul", ps[:, :], wu[n][:, kc, :], u0[:, kc, c0:c0 + TT],
                                        start=(kc == 0), stop=(kc == KC - 1), reads=[bwu[n], bu0[tt + 1]],
                                        writes=[bps] if kc == 0 else [], sig=(kc == KC - 1))
                        kb.mark(tok, reads=[bwu[n]], writes=[bps])
                        kb.op("act", "copy", out=pre[:, ui, 3:3 + TT], in_=ps[:, :], reads=[bps], writes=[bpreu[ui]])
                        kb.op("act", "copy", out=carry[:, ui, :], in_=pre[:, ui, TT:TT + 3], reads=[bpreu[ui]],
                              writes=[bcarry[ui]])
                        a_, ba_ = acc[ui % 2], bacc[ui % 2]
                        cu = cunit[ui]
                        eng = "dve"
                        kb.op(eng, "tensor_scalar", out=a_[:], in0=pre[:, ui, 0:TT], scalar1=cw[:, cu, 0:1], scalar2=None,
                              op0=ALU.mult, reads=[bpreu[ui], bcv], writes=[ba_])
                        for k in range(1, 4):
                            kb.op(eng, "scalar_tensor_tensor", out=a_[:], in0=pre[:, ui, k:k + TT],
                                  scalar=cw[:, cu, k:k + 1], in1=a_[:], op0=ALU.mult, op1=ALU.add,
                                  reads=[bpreu[ui], bcv], writes=[ba_])
                        kb.op("act", "activation", out=xc[:, ui, :], in_=a_[:], func=AF.Silu, bias=cbv[:, cu:cu + 1],
                              reads=[ba_, bcv], writes=[bxc[ui]])
                    if not full:
                        kb.dma("sp", dxcc, dr["xcc"][g, :, :, tt * TT:(tt + 1) * TT].rearrange("u p t -> p u t"),
                               xc[:, 0:5, :], reads=[bxc[0], bxc[1], bxc[2], bxc[3], bxc[4]])
                    if full:
                        for q in range(4):
                            ps, bps = psA[q % 2], bpsA[q % 2]
                            tok = None
                            for kc in range(KC):
                                tok = kb.op("pe", "matmul", ps[:, :], u0[:, kc, c0 + q * 128:c0 + (q + 1) * 128],
                                            wz[:, kc, :], start=(kc == 0), stop=(kc == KC - 1),
                                            reads=[bwz, bu0[tt + 1]], writes=[bps] if kc == 0 else [],
                                            sig=(kc == KC - 1))
                            kb.mark(tok, reads=[bwz], writes=[bps])
                            kb.op("act", "activation", out=zs[:, q, :], in_=ps[:, :], func=AF.Silu, reads=[bps],
                                  writes=[bzs[q]])
                    hs = slice(g * 8, g * 8 + 8)

                    def A_T(q):
                        sl = slice(q * 128, (q + 1) * 128)
                        for u4 in range(4):
                            kb.op("pe", "transpose", out=psT[:, u4, :], in_=xc[:, u4, sl], identity=C.ident[:],
                                  reads=[bxc[u4], C.b], writes=[bpsT] if u4 == 0 else [], sig=False)
                        tok = kb.op("pe", "transpose", out=psT[:, 4, :], in_=xc[:, 4, sl], identity=C.ident[:],
                                    reads=[bxc[4], C.b], writes=[])
                        kb.mark(tok, reads=[bxc[0], bxc[1], bxc[2], bxc[3], bxc[4]], writes=[bpsT])

                    def A_ev(q):
                        b = tt * 4 + q
                        k = q % 2
                        psTx = psT[:, 0:4, :].rearrange("p u c -> p (u c)").rearrange("p (r d) -> p r d", r=8)
                        kb.op("act", "copy", out=btm[k][:], in_=psT[:, 4, :], reads=[bpsT], writes=[bbtm[k]])
                        kb.op("dve", "tensor_tensor", out=xw[k][:].rearrange("p (r c) -> p r c", r=8), in0=psTx,
                              in1=w2m[:, b, hs].unsqueeze(2).to_broadcast([128, 8, 64]), op=ALU.mult,
                              reads=[bpsT, bpre[b]], writes=[bxw[k]])
                        if not full:
                            return
                        kb.op("act", "copy", out=xtm[k][:].rearrange("p (u c) -> p u c", u=4), in_=psT[:, 0:4, :],
                              reads=[bpsT], writes=[bxtm[k]])
                        kb.op("dve", "tensor_tensor", out=xdt[k][:].rearrange("p (r c) -> p r c", r=8), in0=psTx,
                              in1=dtm[:, b, hs].unsqueeze(2).to_broadcast([128, 8, 64]), op=ALU.mult,
                              reads=[bpsT, bpre[b]], writes=[bxdt[k]])

                    def A_cb(q):
                        b = tt * 4 + q
                        sl = slice(q * 128, (q + 1) * 128)
                        kb.op("pe", "matmul", psE[0][:, 0:128], xc[:, 4, sl], xc[:, 5, sl], start=True, stop=True,
                              reads=[bxc[4], bxc[5]], writes=[bpsE[0]])
                        kb.op("dve", "tensor_tensor", out=cbm[:], in0=psE[0][:, 0:128], in1=C.maskT[:], op=ALU.mult,
                              reads=[bpsE[0], C.b], writes=[bcbm])
                        kb.op("dve", "tensor_tensor", out=Ua[:], in0=C.U_f[:].unsqueeze(1).to_broadcast([128, 8, 128]),
                              in1=atm[:, b, hs].unsqueeze(2).to_broadcast([128, 8, 128]), op=ALU.mult,
                              reads=[C.b, bpre[b]], writes=[bUa])
                        for h2 in range(2):
                            kb.op("pe", "matmul", psE[h2][:, :], C.SL_f[:], Ua[:, 4 * h2:4 * h2 + 4, :],
                                  start=True, stop=True, reads=[C.b, bUa], writes=[bpsE[h2]])

                    def A_exp(q):
                        k = q % 2
                        for h2 in range(2):
                            kb.op("act", "activation", out=Ee[:, 4 * h2:4 * h2 + 4, :], in_=psE[h2][:, :],
                                  func=AF.Exp, reads=[bpsE[h2]], writes=[bEe])
                        kb.op("dve", "tensor_tensor", out=Mt[k][:], in0=Ee[:],
                              in1=cbm[:].unsqueeze(1).to_broadcast([128, 8, 128]), op=ALU.mult,
                              reads=[bEe, bcbm], writes=[bMt[k]])

                    def B_mm(q):
                        b = tt * 4 + q
                        k = q % 2
                        sl = slice(q * 128, (q + 1) * 128)
                        if not full:
                            kb.op("pe", "matmul", psAcc[:, :], btm[k][:], xw[k][:], start=(b == 0), stop=(b == NB - 1),
                                  reads=[bbtm[k], bxw[k]], writes=[bpsAcc] if b == 0 else [])
                            if b == NB - 1:
                                kb.mark(kb.cur("pe"), writes=[bpsAcc])
                                kb.op("act", "copy", out=St[:], in_=psAcc[:, :], reads=[bpsAcc], writes=[bSt])
                            return
                        for r in range(8):
                            cs_ = slice(r * 64, (r + 1) * 64)
                            kb.op("pe", "matmul", psY[:, cs_], Mt[k][:, r, :], xdt[k][:, cs_], start=True, stop=False,
                                  reads=[bMt[k], bxdt[k]], writes=[bpsY] if r == 0 else [], sig=False)
                            kb.op("pe", "matmul", psY[:, cs_], idD[:, r, :], xtm[k][:, cs_], start=False, stop=False,
                                  reads=[bidD, bxtm[k]], sig=False)
                            tok = kb.op("pe", "matmul", psY[:, cs_], idD[:, 8 + r, :], xtm[k][:, cs_], start=False,
                                        stop=True, reads=[bidD, bxtm[k]], sig=(r == 7))
                        kb.mark(tok, reads=[bMt[k], bxdt[k], bxtm[k], bidD], writes=[bpsY])
                        kb.op("pe", "matmul", psS[:, :], xc[:, 5, sl], Stb[:], start=True, stop=True,
                              reads=[bxc[5], bStb], writes=[bpsS])
                        kb.op("pe", "matmul", psA[0][:, :], btm[k][:], xw[k][:], start=True, stop=True,
                              reads=[bbtm[k], bxw[k]], writes=[bpsA[0]])

                    def B_ew(q):
                        b = tt * 4 + q
                        if not full:
                            return
                        kb.op("dve", "tensor_tensor", out=Stmp[:].rearrange("p (r c) -> p r c", r=8),
                              in0=St[:].rearrange("p (r c) -> p r c", r=8),
                              in1=dAm[:, b, hs].unsqueeze(2).to_broadcast([128, 8, 64]), op=ALU.mult,
                              reads=[bSt, bpre[b]], writes=[bStmp])
                        kb.op("dve", "tensor_tensor", out=St[:], in0=Stmp[:], in1=psA[0][:, :], op=ALU.add,
                              reads=[bStmp, bpsA[0]], writes=[bSt])
                        kb.op("act", "copy", out=yd[:], in_=psY[:, :], reads=[bpsY], writes=[byd])
                        kb.op("act", "copy", out=Stb[:], in_=St[:], reads=[bSt], writes=[bStb])
                        kb.op("dve", "tensor_tensor", out=yt[:].rearrange("p (r c) -> p r c", r=8),
                              in0=psS[:, :].rearrange("p (r c) -> p r c", r=8),
                              in1=einm[:, b, hs].unsqueeze(2).to_broadcast([128, 8, 64]), op=ALU.mult,
                              reads=[bpsS, bpre[b]], writes=[byt])
                        kb.op("dve", "tensor_tensor", out=yt[:], in0=yt[:], in1=yd[:], op=ALU.add,
                              reads=[byd], writes=[byt])
                        kb.op("dve", "tensor_tensor", out=yt[:], in0=yt[:], in1=zs[:, q, :], op=ALU.mult,
                              reads=[bzs[q]], writes=[byt])
                        kb.op("act", "activation", out=junk[:], in_=yt[:], func=AF.Square, accum_out=st8[:, 0:1],
                              reads=[byt], writes=[bjunk, bst8])
                        kb.op("dve", "tensor_scalar", out=st8[:, 1:2], in0=st8[:, 0:1], scalar1=1.0 / 512,
                              scalar2=EPS, op0=ALU.mult, op1=ALU.add, reads=[bst8], writes=[bst8])
                        kb.op("act", "activation", out=st8[:, 1:2], in_=st8[:, 1:2], func=AF.Ln, reads=[bst8],
                              writes=[bst8])
                        kb.op("act", "activation", out=st8[:, 2:3], in_=st8[:, 1:2], func=AF.Exp, scale=-0.5,
                              reads=[bst8], writes=[bst8])
                        kb.op("act", "activation", out=ynb[:], in_=yt[:], func=AF.Identity, scale=st8[:, 2:3],
                              reads=[byt, bst8], writes=[bynb])

                    def B_T2(q):
                        if not full:
                            return
                        sl = slice(q * 128, (q + 1) * 128)
                        k2 = tt % 2
                        for u4 in range(4):
                            tok = kb.op("pe", "transpose", out=psT2[:, u4, :], in_=ynb[:, u4 * 128:(u4 + 1) * 128],
                                        identity=C.ident[:], reads=[bynb, C.b],
                                        writes=[bpsT2] if u4 == 0 else [], sig=(u4 == 3))
                        kb.mark(tok, reads=[bynb], writes=[bpsT2])
                        for u4 in range(4):
                            if u4 % 2 == 0:
                                kb.op("act", "activation", out=ynT[k2][:, u4, sl], in_=psT2[:, u4, :],
                                      func=AF.Identity, scale=nwc[:, g * 4 + u4:g * 4 + u4 + 1],
                                      reads=[bpsT2, bcv], writes=[bynT[k2]])
                            else:
                                kb.op("dve", "tensor_scalar", out=ynT[k2][:, u4, sl], in0=psT2[:, u4, :],
                                      scalar1=nwc[:, g * 4 + u4:g * 4 + u4 + 1], scalar2=None, op0=ALU.mult,
                                      reads=[bpsT2, bcv], writes=[bynT[k2]])

                    A_T(0)
                    A_ev(0)
                    if full:
                        A_cb(0)
                        A_exp(0)
                    for q in range(4):
                        nq = q + 1 < 4
                        if nq:
                            A_T(q + 1)
                            A_ev(q + 1)
                        B_mm(q)
                        if nq and full:
                            A_cb(q + 1)
                        B_ew(q)
                        if nq and full:
                            A_exp(q + 1)
                        B_T2(q)
                    if full:
                        k2 = tt % 2
                        kb.dma("sp", dout, dr["ynT"][g * 512:(g + 1) * 512, tt * TT:(tt + 1) * TT].rearrange(
                            "(u p) t -> p u t", p=128), ynT[k2][:], reads=[bynT[k2]])
                if not full:
                    kb.dma("sp", dout, dr["Sloc"][g], St[:], reads=[bSt])
            if not full:
                kb.dma("sp", dout, dr["dAblk"], dAm[:], reads=bpre)
            kb.barrier()


def prefix_state(kb, C, dr, tokd=None, toks=None):
    with ExitStack() as es:
        dA = kb.sbuf("pf_dA", [128, NCORES, NB, 64], F32, es=es)
        ws = kb.sbuf("pf_w", [128, 8], F32, es=es)
        dd = kb.sbuf("pf_dd", [128, NCORES, 64], F32, es=es)
        bdA, bws, bdd = Buf("dA"), Buf("ws"), Buf("dd")
        L = [kb.sbuf(f"pf_L{i}", [128, 512], F32, es=es) for i in range(2)]
        bL = [Buf("L0"), Buf("L1")]
        dL = [kb.new_dma_sem("dL0"), kb.new_dma_sem("dL1")]
        Sp = kb.sbuf("pf_S", [128, 512], F32, es=es)
        tmp = kb.sbuf("pf_tmp", [128, 512], F32, es=es)
        bSp, btmp = Buf("Sp"), Buf("tmp")
        d = kb.new_dma_sem("dpf")
        for c in range(NCORES):
            kb.dma("sp", d, dA[:, c], dr["dAall"][c], writes=[bdA], deps=[tokd])
        kb.dma("sp", d, ws[:], dr["wsel"], writes=[bws])
        kb.op("dve", "tensor_copy", out=dd[:], in_=dA[:, :, 0, :], reads=[bdA], writes=[bdd])
        for b in range(1, NB):
            kb.op("dve", "tensor_tensor", out=dd[:], in0=dd[:], in1=dA[:, :, b, :], op=ALU.mult, reads=[bdA],
                  writes=[bdd])
        for c in range(NCORES):
            kb.op("dve", "tensor_scalar", out=dd[:, c, :], in0=dd[:, c, :], scalar1=-1.0, scalar2=ws[:, c:c + 1],
                  op0=ALU.add, op1=ALU.mult, reads=[bws], writes=[bdd])
        kb.op("dve", "tensor_scalar", out=dd[:], in0=dd[:], scalar1=1.0, scalar2=None, op0=ALU.add, writes=[bdd])
        n = 0
        for g in range(NG):
            kb.op("dve", "memset", Sp[:], 0.0, writes=[bSp])
            for c in range(NCORES - 1):
                k = n % 2
                n += 1
                kb.dma("sp", dL[k], L[k][:], dr["Sall"][c][g], writes=[bL[k]],
                       deps=[toks[g // 2]] if toks is not None else [])
                kb.op("dve", "tensor_tensor", out=tmp[:].rearrange("p (r c) -> p r c", r=8),
                      in0=Sp[:].rearrange("p (r c) -> p r c", r=8),
                      in1=dd[:, c, g * 8:g * 8 + 8].unsqueeze(2).to_broadcast([128, 8, 64]), op=ALU.mult,
                      reads=[bSp, bdd], writes=[btmp])
                kb.op("dve", "scalar_tensor_tensor", out=Sp[:], in0=L[k][:], scalar=ws[:, c:c + 1], in1=tmp[:],
                      op0=ALU.mult, op1=ALU.add, reads=[bL[k], bws, btmp], writes=[bSp])
            kb.dma("sp", d, dr["Sinit"][g], Sp[:], reads=[bSp])
        kb.barrier()


def phase_proj_res(kb, C, act_dram, kcin, w_dram, res_dram, res_off, gidx, out_dram):
    with ExitStack() as es:
        S = alloc_norm_scratch(kb, es)
        act = kb.sbuf("pr_act", [128, kcin, TT], BF16, es=es)
        res = kb.sbuf("pr_res", [128, KC, TT], F32, es=es)
        mix = kb.sbuf("pr_mix", [128, KC, TT], F32, es=es)
        bact, bres, bmix = Buf("act"), Buf("res"), Buf("mix")
        ring = WRing(kb, es, "pr_w", 3, kcin, 128)
        ps = [kb.psum(f"pr_ps{i}", [128, TT], es=es) for i in range(2)]
        bps = [PB("ps0"), PB("ps1")]
        d = kb.new_dma_sem("dpr")
        av = act_dram.rearrange("(kc p) t -> p kc t", p=128)
        rv = res_dram.rearrange("(kc p) t -> p kc t", p=128)
        ov = out_dram.rearrange("(kc p) t -> p kc t", p=128)
        pi = 0
        for tt in range(NTT):
            kb.dma("sp", d, act[:], av[:, :, tt * TT:(tt + 1) * TT], writes=[bact])
            kb.dma("sp", d, res[:], rv[:, :, res_off + tt * TT:res_off + (tt + 1) * TT], writes=[bres])

            def evac(j, p_, bp_):
                if j % 2 == 0:
                    kb.op("act", "copy", out=mix[:, j, :], in_=p_[:, :], reads=[bp_], writes=[bmix])
                else:
                    kb.op("dve", "tensor_copy", out=mix[:, j, :], in_=p_[:, :], reads=[bp_], writes=[bmix])
            pi = gemm_fm(kb, ring, ps, bps, act, bact, kcin, TT, w_dram, 0, D, 128, evac, pi0=pi)
            post_norm_residual(kb, C, S, mix, bmix, res, bres, TT, gidx)
            kb.dma("sp", d, ov[:, :, tt * TT:(tt + 1) * TT], res[:], reads=[bres])
        kb.barrier()


def phase_ffn(kb, C, h_dram, layer, wg, wu, wd, out_dram):
    with ExitStack() as es:
        S = alloc_norm_scratch(kb, es)
        res = kb.sbuf("ff_res", [128, KC, TT], F32, es=es)
        mix = kb.sbuf("ff_mix", [128, KC, TT], F32, es=es)
        u2 = kb.sbuf("ff_u2", [128, KC, TT], BF16, es=es)
        a = kb.sbuf("ff_a", [128, FU, TT], BF16, es=es)
        sg = [kb.sbuf(f"ff_sg{i}", [128, TT], F32, es=es) for i in range(2)]
        bres, bmix, bu2, ba = Buf("res"), Buf("mix"), Buf("u2"), Buf("a")
        bsg = [Buf("sg0"), Buf("sg1")]
        ring = WRing(kb, es, "ff_w", 3, FU, 128)
        ringgu = WRing(kb, es, "ff_gu", 6, KC, 128)
        psG = [kb.psum(f"ff_psG{i}", [128, TT], es=es) for i in range(2)]
        psU = [kb.psum(f"ff_psU{i}", [128, TT], es=es) for i in range(2)]
        psD = [kb.psum(f"ff_psD{i}", [128, TT], es=es) for i in range(2)]
        bG, bU, bD = [PB(), PB()], [PB(), PB()], [PB(), PB()]
        d = kb.new_dma_sem("dff")
        hv_ = h_dram.rearrange("(kc p) t -> p kc t", p=128)
        ov = out_dram.rearrange("(kc p) t -> p kc t", p=128)
        pi = 0
        for tt in range(NTT):
            kb.dma("sp", d, res[:], hv_[:, :, tt * TT:(tt + 1) * TT], writes=[bres])
            norm_tile(kb, C, S, res, bres, TT, layer * 4 + 2, u2, bu2)
            for fu in range(FU):
                k = fu % 2
                wgt, wgb = ringgu.load(wg[fu], KC, 128)
                tok = None
                for kc in range(KC):
                    tok = kb.op("pe", "matmul", psG[k][:, :], wgt[:, kc, :], u2[:, kc, :], start=(kc == 0),
                                stop=(kc == KC - 1), reads=[wgb, bu2], writes=[bG[k]] if kc == 0 else [],
                                sig=(kc == KC - 1))
                kb.mark(tok, reads=[wgb, bu2], writes=[bG[k]])
                wut, wub = ringgu.load(wu[fu], KC, 128)
                for kc in range(KC):
                    tok = kb.op("pe", "matmul", psU[k][:, :], wut[:, kc, :], u2[:, kc, :], start=(kc == 0),
                                stop=(kc == KC - 1), reads=[wub, bu2], writes=[bU[k]] if kc == 0 else [],
                                sig=(kc == KC - 1))
                kb.mark(tok, reads=[wub, bu2], writes=[bU[k]])
                kb.op("act", "activation", out=sg[k][:], in_=psG[k][:, :], func=AF.Silu, reads=[bG[k]], writes=[bsg[k]])
                kb.op("dve", "tensor_tensor", out=a[:, fu, :], in0=sg[k][:], in1=psU[k][:, :], op=ALU.mult,
                      reads=[bsg[k], bU[k]], writes=[ba])

            def evac(j, p_, bp_):
                if j % 2 == 0:
                    kb.op("act", "copy", out=mix[:, j, :], in_=p_[:, :], reads=[bp_], writes=[bmix])
                else:
                    kb.op("dve", "tensor_copy", out=mix[:, j, :], in_=p_[:, :], reads=[bp_], writes=[bmix])
            pi = gemm_fm(kb, ring, psD, bD, a, ba, FU, TT, wd, 0, D, 128, evac, pi0=pi)
            post_norm_residual(kb, C, S, mix, bmix, res, bres, TT, layer * 4 + 3)
            kb.dma("sp", d, ov[:, :, tt * TT:(tt + 1) * TT], res[:], reads=[bres])
        kb.barrier()


AT = 512
SLOPES = [2.0 ** (-(h + 1) / 4.0) for h in range(NQH)]


def phase_att(kb, C, dr, h_dram, out_dram):
    with ExitStack() as es:
        S = alloc_norm_scratch(kb, es)
        res = kb.sbuf("at_res", [128, KC, AT], F32, es=es)
        rawA = kb.sbuf("at_rawA", [128, 2 * KC * AT], BF16, es=es)
        mix = rawA[:].bitcast(F32).rearrange("p (k t) -> p k t", k=KC)
        uq = rawA[:, 0:KC * AT].rearrange("p (k t) -> p k t", k=KC)
        ukv = rawA[:, KC * AT:2 * KC * AT].rearrange("p (k t) -> p k t", k=KC)
        qT = kb.sbuf("at_qT", [128, KC, AT], BF16, es=es)
        oT = S.sq
        KT2 = kb.sbuf("at_KT2", [128, NKV, 128 + T], BF16, es=es)
        vtm = kb.sbuf("at_vtm", [128, NB + 1, 256], BF16, es=es)
        wv = kb.sbuf("at_wv", [128, KC, 256], BF16, es=es)
        distm = kb.sbuf("at_dist", [128, 2, 256], F32, es=es)
        sinkb = kb.sbuf("at_sink", [128, NQH], F32, es=es)
        lg = [kb.sbuf(f"at_lg{i}", [128, 8, 256], F32, es=es) for i in range(2)]
        Pb = kb.sbuf("at_P", [128, 8, 256], BF16, es=es)
        PT = kb.sbuf("at_PT", [128, 16, 128], BF16, es=es)
        otm = kb.sbuf("at_otm", [128, 512], BF16, es=es)
        sts = [kb.sbuf(f"at_st{i}", [128, 16, 4], F32, es=es) for i in range(2)]
        bres, bA, bqT = Buf(), Buf(), Buf()
        bmix = buq = bukv = bA
        boT = S.bsq
        bKT = [Buf() for _ in range(NB + 1)]
        bV = [Buf() for _ in range(NB + 1)]
        bwv, bcst, bP, bPT, botm = Buf(), Buf(), Buf(), Buf(), Buf()
        blg = [Buf(), Buf()]
        bsts = [Buf(), Buf()]
        ring = WRing(kb, es, "at_w", 4, KC, 128)
        psA = [kb.psum(f"at_psA{i}", [128, 512], es=es) for i in range(2)]
        bpsA = [PB(), PB()]
        psL = [kb.psum(f"at_psL{i}", [128, 512], es=es) for i in range(2)]
        bpsL = [PB(), PB()]
        psPT = kb.psum("at_psPT", [128, 8, 128], BF16, es=es)
        bpsPT = PB()
        psO = kb.psum("at_psO", [128, 512], es=es)
        bpsO = PB()
        bpsOh = [bpsO, bpsO]
        psOT = kb.psum("at_psOT", [128, 4, 128], BF16, es=es)
        bpsOT = PB()
        d = kb.new_dma_sem("dat")
        kb.dma("sp", d, distm[:], dr["distm"], writes=[bcst])
        kb.dma("sp", d, sinkb[:], dr["sinks"], writes=[bcst])
        kb.dma("pool", d, wv[:], dr["w_v"], writes=[bwv])
        hv_ = h_dram.rearrange("(kc p) t -> p kc t", p=128)
        ov = out_dram.rearrange("(kc p) t -> p kc t", p=128)
        pi = 0

        def kv_for(N, blk0, nblk):
            nonlocal pi

            def evk(j, p_, bp_):
                for q in range(nblk):
                    kb.op("act", "copy", out=KT2[:, j, (blk0 + q) * 128:(blk0 + q + 1) * 128],
                          in_=p_[:, q * 128:(q + 1) * 128], reads=[bp_], writes=[bKT[blk0 + q]])
            pi = gemm_fm(kb, ring, psA, bpsA, ukv, bukv, KC, N, dr["wk_dup"], 0, 512, 128, evk, pi0=pi)
            for q in range(nblk):
                ps, bps = psA[pi % 2], bpsA[pi % 2]
                pi += 1
                tok = None
                for kc in range(KC):
                    tok = kb.op("pe", "matmul", ps[:, 0:256], ukv[:, kc, q * 128:(q + 1) * 128], wv[:, kc, :],
                                start=(kc == 0), stop=(kc == KC - 1), reads=[bukv, bwv],
                                writes=[bps] if kc == 0 else [], sig=(kc == KC - 1))
                kb.mark(tok, reads=[bukv, bwv], writes=[bps])
                kb.op("dve", "tensor_copy", out=vtm[:, blk0 + q, :], in_=ps[:, 0:256], reads=[bps], writes=[bV[blk0 + q]])

        kb.dma("sp", d, res[:, :, 0:128], hv_[:, :, 0:128], writes=[bres])
        norm_tile(kb, C, S, res, bres, 128, 8, ukv, bukv)
        kv_for(128, 0, 1)
        for tt in range(T // AT):
            kb.dma("sp", d, res[:], hv_[:, :, 128 + tt * AT:128 + (tt + 1) * AT], writes=[bres])
            norm_tile(kb, C, S, res, bres, AT, 4, uq, buq, gidx2=8, out_bf2=ukv, bout2=bukv)
            kv_for(AT, 1 + tt * (AT // 128), AT // 128)

            def evq(j, p_, bp_):
                kb.op("act", "activation", out=qT[:, j, :], in_=p_[:, :AT], func=AF.Copy, scale=0.125,
                      reads=[bp_], writes=[bqT])
            pi = gemm_fm(kb, ring, psA, bpsA, uq, buq, KC, AT, dr["w_q"], 0, D, 128, evq, pi0=pi)
            rounds = [(qb, kvg) for qb in range(AT // 128) for kvg in range(NKV)]

            def R_L(ri):
                qb, kvg = rounds[ri]
                k = ri % 2
                bi = tt * (AT // 128) + qb
                qs = slice(qb * 128, (qb + 1) * 128)
                dsel = 0 if bi == 0 else 1
                for h8 in range(8):
                    h = kvg * 8 + h8
                    unit, half = h // 2, h % 2
                    pl, bpl = psL[h8 % 2], bpsL[h8 % 2]
                    kb.op("pe", "matmul", pl[:, 0:256],
                          qT[half * 64:(half + 1) * 64, unit, qs],
                          KT2[half * 64:(half + 1) * 64, kvg, bi * 128:bi * 128 + 256], start=True, stop=True,
                          reads=[bqT, bKT[bi], bKT[bi + 1]], writes=[bpl])
                    kb.op("dve", "scalar_tensor_tensor", out=lg[k][:, h8, :], in0=distm[:, dsel, :],
                          scalar=-SLOPES[h], in1=pl[:, 0:256], op0=ALU.mult,
                          op1=ALU.add, reads=[bcst, bpl], writes=[blg[k]])
                hs = slice(kvg * 8, kvg * 8 + 8)
                st = sts[k]
                kb.op("dve", "tensor_reduce", out=st[:, 0:8, 0], in_=lg[k][:], axis=AX.X, op=ALU.max,
                      reads=[blg[k]], writes=[bsts[k]])
                kb.op("dve", "tensor_tensor", out=st[:, 0:8, 0], in0=st[:, 0:8, 0], in1=sinkb[:, hs], op=ALU.max,
                      reads=[bcst], writes=[bsts[k]])
                kb.op("dve", "tensor_scalar", out=st[:, 0:8, 1], in0=st[:, 0:8, 0], scalar1=-1.0, scalar2=None,
                      op0=ALU.mult, writes=[bsts[k]])
                kb.op("dve", "tensor_tensor", out=st[:, 8:16, 0], in0=sinkb[:, hs], in1=st[:, 0:8, 0],
                      op=ALU.subtract, reads=[bcst], writes=[bsts[k]])

            def R_E(ri):
                k = ri % 2
                st = sts[k]
                for h8 in range(8):
                    kb.op("act", "activation", out=Pb[:, h8, :], in_=lg[k][:, h8, :], func=AF.Exp,
                          bias=st[:, h8, 1:2], accum_out=st[:, h8, 2:3], reads=[blg[k], bsts[k]],
                          writes=[bP, bsts[k]])
                kb.op("act", "activation", out=st[:, 8:16, 1], in_=st[:, 8:16, 0], func=AF.Exp, writes=[bsts[k]])

            def R_T(ri):
                for hh in range(2):
                    for h4 in range(4):
                        for hf in range(2):
                            tok = kb.op("pe", "transpose", out=psPT[:, h4 * 2 + hf, :],
                                        in_=Pb[:, hh * 4 + h4, hf * 128:(hf + 1) * 128], identity=C.ident[:],
                                        reads=[bP, C.b], writes=[bpsPT] if (h4 == 0 and hf == 0) else [],
                                        sig=(h4 == 3 and hf == 1))
                    kb.mark(tok, reads=[bP], writes=[bpsPT])
                    if hh == 0:
                        kb.op("act", "copy", out=PT[:, 0:8, :], in_=psPT[:, :, :], reads=[bpsPT], writes=[bPT])
                    else:
                        kb.op("dve", "tensor_copy", out=PT[:, 8:16, :], in_=psPT[:, :, :], reads=[bpsPT],
                              writes=[bPT])

            def R_V(ri):
                qb, kvg = rounds[ri]
                k = ri % 2
                st = sts[k]
                bi = tt * (AT // 128) + qb
                qs = slice(qb * 128, (qb + 1) * 128)
                for h8 in range(8):
                    c0 = h8 * 64
                    for hf in range(2):
                        tok = kb.op("pe", "matmul", psO[:, c0:c0 + 64], PT[:, h8 * 2 + hf, :],
                                    vtm[:, bi + hf, kvg * 64:(kvg + 1) * 64], start=(hf == 0), stop=(hf == 1),
                                    reads=[bPT, bV[bi], bV[bi + 1]],
                                    writes=[bpsO] if (h8 == 0 and hf == 0) else [],
                                    sig=(h8 == 7 and hf == 1))
                kb.mark(tok, reads=[bPT], writes=[bpsO])
                kb.op("dve", "tensor_tensor", out=st[:, 8:16, 2], in0=st[:, 8:16, 1], in1=st[:, 0:8, 2], op=ALU.add,
                      writes=[bsts[k]])
                kb.op("dve", "reciprocal", out=st[:, 8:16, 3], in_=st[:, 8:16, 2], writes=[bsts[k]])
                kb.op("dve", "tensor_tensor", out=otm[:, :].rearrange("p (r c) -> p r c", r=8),
                      in0=psO[:, :].rearrange("p (r c) -> p r c", r=8),
                      in1=st[:, 8:16, 3:4].to_broadcast([128, 8, 64]), op=ALU.mult,
                      reads=[bsts[k], bpsO], writes=[botm])
                for u4 in range(4):
                    tok = kb.op("pe", "transpose", out=psOT[:, u4, :], in_=otm[:, u4 * 128:(u4 + 1) * 128],
                                identity=C.ident[:], reads=[botm, C.b], writes=[bpsOT] if u4 == 0 else [],
                                sig=(u4 == 3))
                kb.mark(tok, reads=[botm], writes=[bpsOT])
                kb.op("act", "copy", out=oT[:, kvg * 4:(kvg + 1) * 4, qs], in_=psOT[:, :, :], reads=[bpsOT],
                      writes=[boT])

            R_L(0)
            for ri in range(len(rounds)):
                R_E(ri)
                if ri + 1 < len(rounds):
                    R_L(ri + 1)
                R_T(ri)
                R_V(ri)

            def evo(j, p_, bp_):
                if j % 2 == 0:
                    kb.op("act", "copy", out=mix[:, j, :], in_=p_[:, :AT], reads=[bp_], writes=[bmix])
                else:
                    kb.op("dve", "tensor_copy", out=mix[:, j, :], in_=p_[:, :AT], reads=[bp_], writes=[bmix])
            pi = gemm_fm(kb, ring, psA, bpsA, oT, boT, KC, AT, dr["w_o"], 0, D, 128, evo, pi0=pi)
            post_norm_residual(kb, C, S, mix, bmix, res, bres, AT, 5)
            kb.dma("sp", d, ov[:, :, tt * AT:(tt + 1) * AT], res[:], reads=[bres])
        kb.barrier()


CONST_SPECS = [("ident", [128, 128], BF16), ("ones_bf", [128, 128], BF16), ("maskT", [128, 128], BF16),
               ("ones_f", [128, 128], F32), ("U_f", [128, 128], F32), ("SL_f", [128, 128], F32),
               ("ncols", [128, 9, 16], F32)]


def _decl(nc, dr, name, shape, dt, kind):
    dr[name] = nc.dram_tensor(name, list(shape), dt, kind=kind).ap()


def _finish(kb):
    kb.barrier()


def _consts(norm_w, kv_norm_w):
    bf = ml_dtypes.bfloat16
    i = np.arange(128)
    c = {}
    c["ident"] = np.eye(128, dtype=np.float32).astype(bf)
    c["ones_bf"] = np.ones((128, 128), np.float32).astype(bf)
    c["maskT"] = (i[:, None] <= i[None, :]).astype(np.float32).astype(bf)
    c["ones_f"] = np.ones((128, 128), np.float32)
    c["U_f"] = (i[:, None] <= i[None, :]).astype(np.float32)
    c["SL_f"] = (i[:, None] > i[None, :]).astype(np.float32)
    g = np.concatenate([norm_w.reshape(8, D), kv_norm_w.reshape(1, D)], axis=0)
    c["ncols"] = np.ascontiguousarray(g.reshape(9, KC, 128).transpose(2, 0, 1))
    return c


def _bc(v, n=128):
    return np.ascontiguousarray(np.broadcast_to(np.asarray(v, np.float32)[None], (n,) + tuple(np.shape(v))))


_CACHE = {}


def _prog(name, fn):
    if name not in _CACHE:
        _CACHE[name] = fn()
    return _CACHE[name]


def all_gather8(kb, nc, cc, src, mid, dst, deps):
    kb._wait("pool", deps)
    g4 = [[0, 1, 2, 3], [4, 5, 6, 7]]
    g2 = [[i, i + 4] for i in range(4)]
    inst = nc.gpsimd.collective_compute("AllGather", ALU.bypass, replica_groups=g4, ins=[src.opt()], outs=[mid.opt()])
    cc[1] += 1
    inst.then_inc(cc[0], 1)
    nc.gpsimd.wait_ge(cc[0], cc[1])
    inst = nc.gpsimd.collective_compute("AllGather", ALU.bypass, replica_groups=g2, ins=[mid.opt()], outs=[dst.opt()])
    cc[1] += 1
    inst.then_inc(cc[0], 1)
    nc.gpsimd.wait_ge(cc[0], cc[1])
    return (cc[0], cc[1])


def halo_select(kb, C, dr, h8, h2h):
    with ExitStack() as es:
        hs = kb.sbuf("hs_sel", [128, 8], F32, es=es)
        acc = kb.sbuf("hs_acc", [128, KC, 128], F32, es=es)
        t = [kb.sbuf(f"hs_t{i}", [128, KC, 128], F32, es=es) for i in range(2)]
        bt = [Buf(), Buf()]
        dt_ = [kb.new_dma_sem("dhs0"), kb.new_dma_sem("dhs1")]
        bhs, bacc = Buf(), Buf()
        d = kb.new_dma_sem("dhs")
        kb.dma("sp", d, hs[:], dr["hsel"], writes=[bhs])
        kb.op("dve", "memset", acc[:], 0.0, writes=[bacc])
        for j in range(NCORES):
            k = j % 2
            for q in range(2):
                kb.dma("sp", dt_[k], t[k][:, q * 8:(q + 1) * 8, :],
                       h8[q][j * 1024:(j + 1) * 1024, :].rearrange("(kc p) t -> p kc t", p=128), writes=[bt[k]])
            kb.op("dve", "scalar_tensor_tensor", out=acc[:], in0=t[k][:], scalar=hs[:, j:j + 1], in1=acc[:],
                  op0=ALU.mult, op1=ALU.add, reads=[bt[k], bhs], writes=[bacc])
        kb.dma("sp", d, h2h[:, 0:128].rearrange("(kc p) t -> p kc t", p=128), acc[:], reads=[bacc])
        kb.barrier()


def build_fused():
    nc = bass.Bass("TRN2", target_bir_lowering=False)
    dr = {}
    for n, sh, dt in CONST_SPECS:
        _decl(nc, dr, n, sh, dt, "ExternalInput")
    ext = [("xT", [D, CH + T]), ("hv", [128, 3, 64]), ("w_in_u", [NG, 6, 128, KC, 128]),
           ("w_in_z", [NG, 128, KC, 512]), ("w_in_dt", [128, KC, 64]), ("conv_w", [128, 48, 4]),
           ("conv_b", [128, 48]), ("ssm_nw", [128, 32]), ("wsel", [128, 8]), ("hsel", [128, 8]),
           ("w_out", [16, 128, DI // 128, 128]), ("wg0", [FU, 128, KC, 128]), ("wu0", [FU, 128, KC, 128]),
           ("wd0", [16, 128, FU, 128]), ("w_q", [16, 128, KC, 128]), ("w_o", [16, 128, KC, 128]),
           ("wk_dup", [4, 128, KC, 128]), ("w_v", [128, KC, 256]),
           ("distm", [128, 2, 256]), ("sinks", [128, NQH]), ("wg1", [FU, 128, KC, 128]), ("wu1", [FU, 128, KC, 128]),
           ("wd1", [16, 128, FU, 128])]
    for n, sh in ext:
        _decl(nc, dr, n, sh, F32, "ExternalInput")
    _decl(nc, dr, "outT", [D, T], F32, "ExternalOutput")
    pkg = [nc.dram_tensor(f"pk_{g}", [256, 512], F32).ap() for g in range(NG // 2)]
    pkg4 = [nc.dram_tensor(f"pk4_{g}", [4 * 256, 512], F32).ap() for g in range(NG // 2)]
    pkg8 = [nc.dram_tensor(f"pk8_{g}", [8 * 256, 512], F32).ap() for g in range(NG // 2)]
    pkd = nc.dram_tensor("pkd", [256, 512], F32).ap()
    pkd4 = nc.dram_tensor("pkd4", [4 * 256, 512], F32).ap()
    pkd8 = nc.dram_tensor("pkd8", [8 * 256, 512], F32).ap()
    hsl = [nc.dram_tensor(f"hsl_{q}", [1024, 128], F32).ap() for q in range(2)]
    hsl4 = [nc.dram_tensor(f"hsl4_{q}", [4 * 1024, 128], F32).ap() for q in range(2)]
    hsl8 = [nc.dram_tensor(f"hsl8_{q}", [8 * 1024, 128], F32).ap() for q in range(2)]
    _decl(nc, dr, "Sinit", [NG, 128, 512], F32, "Internal")
    _decl(nc, dr, "ynT", [DI, T], BF16, "Internal")
    _decl(nc, dr, "h1T", [D, T], F32, "Internal")
    _decl(nc, dr, "h2h", [D, 128 + T], F32, "Internal")
    _decl(nc, dr, "h3T", [D, T], F32, "Internal")
    _decl(nc, dr, "xcc", [NG, 5, 128, T], BF16, "Internal")
    _decl(nc, dr, "u0c", [128, KC, CH + T], BF16, "Internal")
    _decl(nc, dr, "prec", [5, 128, NB, 64], F32, "Internal")

    def dAview(ap_rows):
        return ap_rows.rearrange("(p two) f -> p (two f)", two=2).rearrange("p (b h) -> p b h", b=NB)
    dr["Sloc"] = [pkg[g // 2][(g % 2) * 128:(g % 2 + 1) * 128, :] for g in range(NG)]
    dr["dAblk"] = dAview(pkd)
    dr["Sall"] = [[pkg8[g // 2][c * 256 + (g % 2) * 128:c * 256 + (g % 2 + 1) * 128, :] for g in range(NG)]
                  for c in range(NCORES)]
    dr["dAall"] = [dAview(pkd8[c * 256:(c + 1) * 256, :]) for c in range(NCORES)]
    with ExitStack() as es:
        kb = KB(nc, es)
        kb.nsem += 1
        cc = [es.enter_context(nc.semaphore("ccsem")), 0]
        C = load_consts(kb, dr)
        ssm_pass(kb, C, dr, full=False)
        kb.barrier()
        tokd = all_gather8(kb, nc, cc, pkd, pkd4, pkd8, [])
        toks = [all_gather8(kb, nc, cc, pkg[g], pkg4[g], pkg8[g], []) for g in range(NG // 2)]
        prefix_state(kb, C, dr, tokd, toks)
        ssm_pass(kb, C, dr, full=True)
        phase_proj_res(kb, C, dr["ynT"], DI // 128, dr["w_out"], dr["xT"], CH, 1, dr["h1T"])
        phase_ffn(kb, C, dr["h1T"], 0, dr["wg0"], dr["wu0"], dr["wd0"], dr["h2h"][:, 128:128 + T])
        dh = kb.new_dma_sem("dhsl")
        for q in range(2):
            kb.dma("sp", dh, hsl[q], dr["h2h"][q * 1024:(q + 1) * 1024, T:T + 128])
        kb.barrier()
        for q in range(2):
            tok = all_gather8(kb, nc, cc, hsl[q], hsl4[q], hsl8[q], [])
        kb.barrier(extra=[tok])
        halo_select(kb, C, dr, hsl8, dr["h2h"])
        phase_att(kb, C, dr, dr["h2h"], dr["h3T"])
        phase_ffn(kb, C, dr["h3T"], 1, dr["wg1"], dr["wu1"], dr["wd1"], dr["outT"])
        _finish(kb)
    return nc


def kernel(x, norm_w, ssm_w_in, ssm_conv_w, ssm_conv_b, ssm_dt_bias, ssm_A_log, ssm_D, ssm_norm_w, ssm_w_out,
           kv_norm_w, w_kv, attn_w_q, attn_sinks, attn_w_o, ffn_w_gate, ffn_w_up, ffn_w_down):
    f32 = np.float32
    x = np.asarray(x, f32)
    cores = list(range(NCORES))
    base = _consts(np.asarray(norm_w, f32), np.asarray(kv_norm_w, f32))
    xpad = np.concatenate([np.zeros((CH, D), f32), x[0]], axis=0)
    hv = np.stack([_bc(np.asarray(ssm_dt_bias, f32)[0]), _bc(np.asarray(ssm_A_log, f32)[0]),
                   _bc(np.asarray(ssm_D, f32)[0])], axis=1)
    wg = np.asarray(ffn_w_gate, f32)
    wu = np.asarray(ffn_w_up, f32)
    wd = np.asarray(ffn_w_down, f32)
    wkv = np.asarray(w_kv, f32)
    wk = wkv[:, :256].reshape(D, NKV, 1, HD)
    qi = np.arange(128)[:, None]
    sj = np.arange(256)[None, :]
    dist = (128 + qi - sj).astype(f32)
    BIG = 1.0e6
    dm = np.where((dist >= 0) & (dist < WIN), dist, BIG).astype(f32)
    dm_first = dm.copy()
    dm_first[:, :128] = BIG
    def tw(w, uw):
        K_, N_ = w.shape
        return np.ascontiguousarray(w.reshape(K_ // 128, 128, N_ // uw, uw).transpose(2, 1, 0, 3))
    w_in = np.asarray(ssm_w_in, f32)[0]
    w_in_x = tw(w_in[:, C_X:C_X + DI], 128).reshape(NG, 4, 128, KC, 128)
    w_in_B = tw(w_in[:, C_B:C_B + 1024], 128).reshape(NG, 1, 128, KC, 128)
    w_in_C = tw(w_in[:, C_C:C_C + 1024], 128).reshape(NG, 1, 128, KC, 128)
    shared = dict(
        base, hv=hv, w_in_u=np.ascontiguousarray(np.concatenate([w_in_x, w_in_B, w_in_C], axis=1)),
        w_in_z=tw(w_in[:, C_Z:C_Z + DI], 512), w_in_dt=tw(w_in[:, C_DT:C_DT + 64], 64)[0],
        conv_w=np.ascontiguousarray(np.asarray(ssm_conv_w, f32)[0].reshape(4, 48, 128).transpose(2, 1, 0)),
        conv_b=np.ascontiguousarray(np.asarray(ssm_conv_b, f32)[0].reshape(48, 128).T),
        ssm_nw=np.ascontiguousarray(np.asarray(ssm_norm_w, f32)[0].reshape(32, 128).T),
        w_out=tw(np.asarray(ssm_w_out, f32)[0], 128),
        wg0=tw(wg[0], 128), wu0=tw(wu[0], 128), wd0=tw(wd[0], 128),
        wg1=tw(wg[1], 128), wu1=tw(wu[1], 128), wd1=tw(wd[1], 128),
        w_q=tw(np.asarray(attn_w_q, f32)[0], 128), w_o=tw(np.asarray(attn_w_o, f32)[0], 128),
        wk_dup=tw(np.ascontiguousarray(np.broadcast_to(wk, (D, NKV, 2, HD)).reshape(D, 512)), 128),
        w_v=tw(np.ascontiguousarray(wkv[:, 256:]), 256)[0], sinks=_bc(np.asarray(attn_sinks, f32)[0]))
    in_maps = []
    for c in cores:
        in_maps.append(dict(
            shared, xT=np.ascontiguousarray(xpad[c * T:c * T + CH + T].T),
            wsel=_bc((np.arange(8) < c).astype(f32)), hsel=_bc((np.arange(8) == c - 1).astype(f32)),
            distm=np.ascontiguousarray(np.stack([dm_first if c == 0 else dm, dm], axis=1))))
    nc = _prog("fused", build_fused)
    res = run_bass_kernel_spmd(nc, in_maps, core_ids=cores).results
    out = np.concatenate([res[c]["outT"].T for c in cores], axis=0)
    return np.ascontiguousarray(out[None]).astype(f32)
```
